# Optimizing a Trainium2 kernel written in Bass

```python
import jax, jax.numpy as jnp
from jax import lax
import numpy as np

D_MODEL = 1024
BATCH = 8
SEQ = 4096
DEPTH = 4

GRID_W = 64
CTX_LEN = 256
N_MIXERS = 2
N_FOURIER_LAYERS = (DEPTH + 1) // 2
N_HGRN_LAYERS = DEPTH // 2
FOURIER_GROUPS = 4
FOURIER_GROUP_DIM = D_MODEL // FOURIER_GROUPS
HGRN_HEADS = 8
HGRN_DK = D_MODEL // HGRN_HEADS
HGRN_DV = D_MODEL // HGRN_HEADS
HGRN_KD = HGRN_HEADS * HGRN_DK
HGRN_VD = HGRN_HEADS * HGRN_DV
HGRN_PROJ = 3 * HGRN_KD + 2 * HGRN_VD
HGRN_SPLITS = (HGRN_KD, HGRN_KD + HGRN_VD, 2 * HGRN_KD + HGRN_VD, 3 * HGRN_KD + HGRN_VD)
HGRN_CHUNK = 32
LB_FLOOR = 1e-30
D_FF = 2816
N_SUB = 3
N_MOD = 3 * N_SUB
HALF = 0.5
NORM_EPS = 1e-6

kernel_name = "hybrid_fourier_hgrn2_macaron_dit"


def rms_norm(x, g):
    xf = x.astype(jnp.float32)
    y = xf * lax.rsqrt(jnp.mean(xf * xf, axis=-1, keepdims=True) + NORM_EPS)
    return (y * g.astype(jnp.float32)).astype(x.dtype)


def pre_sub(s, mod, j, g):
    return rms_norm(s, g) * (1 + mod[:, :, 3 * j + 1]) + mod[:, :, 3 * j]


def post_sub(s, y, mod, j, g, w):
    return s + w * mod[:, :, 3 * j + 2] * rms_norm(y, g)


def swiglu(h, w_in, w_out):
    gate, up = jnp.split(h @ w_in, 2, axis=-1)
    return (jax.nn.silu(gate) * up) @ w_out


def fourier_grid(h, rows):
    B, L, D = h.shape
    hf = h.astype(jnp.float32).reshape(B, rows, GRID_W, FOURIER_GROUPS, FOURIER_GROUP_DIM)
    y = jnp.fft.fftn(hf, axes=(1, 2, 4), norm="ortho").real
    return y.reshape(B, L, D).astype(h.dtype)


def fourier_seq(h):
    B, L, D = h.shape
    hf = h.astype(jnp.float32).reshape(B, L, FOURIER_GROUPS, FOURIER_GROUP_DIM)
    y = jnp.fft.fftn(hf, axes=(1, 3), norm="ortho").real
    return y.reshape(B, L, D).astype(h.dtype)


def hgrn_lower_bound(lb_logits, j):
    p = jax.nn.softmax(lb_logits.astype(jnp.float32), axis=0)
    lb = jnp.cumsum(p, axis=0) - p[0]
    return lb[j]


def forget_gate(z, lb):
    lb = lb.reshape(HGRN_HEADS, 1, HGRN_DK)
    z = z.astype(jnp.float32)
    log_f = jnp.logaddexp(jnp.log(jnp.maximum(lb, LB_FLOOR)), jnp.log1p(-lb) + jax.nn.log_sigmoid(z))
    k = (1 - lb) * jax.nn.sigmoid(-z)
    return k, log_f


def hgrn2_inputs(h, w_in, lb_f, lb_b):
    B, L, _ = h.shape
    q, v, zf, zb, g = jnp.split(h @ w_in, HGRN_SPLITS, axis=-1)
    heads = lambda a: a.reshape(B, L, HGRN_HEADS, -1).transpose(0, 2, 1, 3).astype(jnp.float32)
    q = jax.nn.silu(heads(q))
    kf, lf = forget_gate(heads(zf), lb_f)
    kb, lbw = forget_gate(heads(zb), lb_b)
    return q, heads(v), kf, lf, kb, lbw, g


def gla_chunk_scan(q, k, v, log_f, s0):
    B, H, L, _ = q.shape
    n = L // HGRN_CHUNK
    to_chunks = lambda a: jnp.moveaxis(a.reshape(B, H, n, HGRN_CHUNK, a.shape[-1]), 2, 0)
    incl = jnp.tril(jnp.ones((HGRN_CHUNK, HGRN_CHUNK), bool))[:, :, None]

    def step(s, inp):
        qi, ki, vi, gi = inp
        b = jnp.cumsum(gi, axis=2)
        diff = b[:, :, :, None, :] - b[:, :, None, :, :]
        decay = jnp.where(incl, jnp.exp(jnp.where(incl, diff, 0.0)), 0.0)
        a = jnp.einsum('bhtk,bhsk,bhtsk->bhts', qi, ki, decay)
        o = (jnp.einsum('bhts,bhsv->bhtv', a, vi)
             + jnp.einsum('bhtk,bhkv->bhtv', qi * jnp.exp(b), s))
        b_last = b[:, :, -1:, :]
        s_new = (jnp.exp(b_last[:, :, 0, :])[..., None] * s
                 + jnp.einsum('bhsk,bhsv->bhkv', ki * jnp.exp(b_last - b), vi))
        return s_new, o

    s_fin, oc = lax.scan(step, s0, (to_chunks(q), to_chunks(k), to_chunks(v), to_chunks(log_f)))
    o = jnp.moveaxis(oc, 0, 2).reshape(B, H, L, -1)
    return o, s_fin


def bidir_scan(q, kf, lf, kb, lbw, v, s_f, s_b):
    flip = lambda a: jnp.flip(a, axis=2)
    o_f, s_f_out = gla_chunk_scan(q, kf, v, lf, s_f)
    o_b, s_b_out = gla_chunk_scan(flip(q), flip(kb), flip(v), flip(lbw), s_b)
    return o_f + flip(o_b), s_f_out, s_b_out


def hgrn2_readout(o, g, g_norm, w_out, dtype):
    B, H, L, _ = o.shape
    o = rms_norm(o, g_norm).transpose(0, 2, 1, 3).reshape(B, L, HGRN_VD).astype(dtype)
    return (o * jax.nn.silu(g)) @ w_out


def setup_inputs(seed: int = 0) -> dict:
    key = jax.random.key(seed)
    ks = jax.random.split(key, 16)
    d = D_MODEL
    nrm = lambda k, shape, scale: scale * jax.random.normal(k, shape, jnp.float32)
    return {
        "x": nrm(ks[0], (BATCH, SEQ, d), 1.0),
        "c": nrm(ks[1], (BATCH, d), 1.0),
        "ctx": nrm(ks[2], (BATCH, CTX_LEN, d), 1.0),
        "c_ctx": nrm(ks[3], (d,), 1.0),
        "ada_w": nrm(ks[4], (DEPTH, d, N_MOD * d), 0.5 * d ** -0.5),
        "ada_b": nrm(ks[5], (DEPTH, N_MOD * d), 0.02),
        "norm_pre": 1.0 + nrm(ks[6], (DEPTH, N_SUB, d), 0.05),
        "norm_post": 1.0 + nrm(ks[7], (DEPTH, N_SUB, d), 0.05),
        "ffn_w_in": nrm(ks[8], (DEPTH, 2, d, 2 * D_FF), d ** -0.5),
        "ffn_w_out": nrm(ks[9], (DEPTH, 2, D_FF, d), D_FF ** -0.5),
        "fourier_w_out": nrm(ks[10], (N_FOURIER_LAYERS, d, d), d ** -0.5),
        "hgrn_w_in": nrm(ks[11], (N_HGRN_LAYERS, d, HGRN_PROJ), d ** -0.5),
        "hgrn_lb_fwd": nrm(ks[12], (N_HGRN_LAYERS, HGRN_KD), 0.5),
        "hgrn_lb_bwd": nrm(ks[13], (N_HGRN_LAYERS, HGRN_KD), 0.5),
        "hgrn_norm": 1.0 + nrm(ks[14], (N_HGRN_LAYERS, HGRN_DV), 0.05),
        "hgrn_w_out": nrm(ks[15], (N_HGRN_LAYERS, HGRN_VD, d), HGRN_VD ** -0.5),
    }


def reference(x, c, ctx, c_ctx, ada_w, ada_b, norm_pre, norm_post, ffn_w_in, ffn_w_out,
              fourier_w_out, hgrn_w_in, hgrn_lb_fwd, hgrn_lb_bwd, hgrn_norm, hgrn_w_out):
    B, L, D = x.shape
    rows = L // GRID_W
    sc, sctx = jax.nn.silu(c), jax.nn.silu(c_ctx)
    for i in range(DEPTH):
        last = i == DEPTH - 1
        is_hgrn = i % N_MIXERS == 1
        jm = i // N_MIXERS
        ctx_in = is_hgrn or not last
        mx = (sc @ ada_w[i] + ada_b[i]).reshape(B, 1, N_MOD, D)
        mc = (sctx @ ada_w[i] + ada_b[i]).reshape(1, 1, N_MOD, D)

        def ffn_sub(s, mod, j, f):
            h = pre_sub(s, mod, j, norm_pre[i, j])
            return post_sub(s, swiglu(h, ffn_w_in[i, f], ffn_w_out[i, f]), mod, j, norm_post[i, j], HALF)

        x = ffn_sub(x, mx, 0, 0)
        if ctx_in:
            ctx = ffn_sub(ctx, mc, 0, 0)

        hx = pre_sub(x, mx, 1, norm_pre[i, 1])
        if not is_hgrn:
            yx = fourier_grid(hx, rows) @ fourier_w_out[jm]
            x = post_sub(x, yx, mx, 1, norm_post[i, 1], 1)
            if not last:
                hc = pre_sub(ctx, mc, 1, norm_pre[i, 1])
                yc = fourier_seq(hc) @ fourier_w_out[jm]
                ctx = post_sub(ctx, yc, mc, 1, norm_post[i, 1], 1)
        else:
            lb_f = hgrn_lower_bound(hgrn_lb_fwd, jm)
            lb_b = hgrn_lower_bound(hgrn_lb_bwd, jm)
            hc = pre_sub(ctx, mc, 1, norm_pre[i, 1])
            qc, vc, kfc, lfc, kbc, lbc, gc = hgrn2_inputs(hc, hgrn_w_in[jm], lb_f, lb_b)
            zero = jnp.zeros((B, HGRN_HEADS, HGRN_DK, HGRN_DV), jnp.float32)
            oc, s_f, s_b = bidir_scan(qc, kfc, lfc, kbc, lbc, vc, zero, zero)
            qx, vx, kfx, lfx, kbx, lbx, gx = hgrn2_inputs(hx, hgrn_w_in[jm], lb_f, lb_b)
            ox, _, _ = bidir_scan(qx, kfx, lfx, kbx, lbx, vx, s_f, s_b)
            yx = hgrn2_readout(ox, gx, hgrn_norm[jm], hgrn_w_out[jm], x.dtype)
            x = post_sub(x, yx, mx, 1, norm_post[i, 1], 1)
            if not last:
                yc = hgrn2_readout(oc, gc, hgrn_norm[jm], hgrn_w_out[jm], ctx.dtype)
                ctx = post_sub(ctx, yc, mc, 1, norm_post[i, 1], 1)

        x = ffn_sub(x, mx, 2, 1)
        if not last:
            ctx = ffn_sub(ctx, mc, 2, 1)
    return x
```

```python
import bisect
import contextlib
import numpy as np
import concourse.bass as bass
import concourse.mybir as mybir
from concourse.bass_utils import run_bass_kernel_spmd

F32 = mybir.dt.float32
BF16 = mybir.dt.bfloat16
AF = mybir.ActivationFunctionType
ALU = mybir.AluOpType

D = 1024
L = 4096
LC = 256
DEPTH = 4
DFF = 2816
NFC = DFF // 128
EPS = 1e-6
CH = 64
MID = CH // 2 - 1


class Buf:
    __slots__ = ("name", "lw", "rd")

    def __init__(self, name):
        self.name = name
        self.lw = None
        self.rd = {}


class T:
    def __init__(self, t, name):
        self.t = t
        self.b = Buf(name)

    def __getitem__(self, idx):
        return self.t[idx]


ENGS = ("pe", "act", "dve", "pool", "sp")
SAME_ENG_SYNC = True
NDS = 48
STREAMS = tuple(f"d{k}" for k in range(NDS))


class Sched:
    def __init__(self, nc, es):
        self.nc = nc
        self.keys = list(ENGS) + list(STREAMS)
        self.sem = {k: es.enter_context(nc.semaphore("s_" + k)) for k in self.keys}
        self.cnt = {k: 0 for k in self.keys}
        self.base = {k: 0 for k in self.keys}
        self.sigbase = {k: 0 for k in self.keys}
        self.q = {e: [] for e in ENGS}
        self.sig = {k: set() for k in self.keys}
        self.last_dma_eng = {}
        self.bufkey = {}

    def op(self, eng, meth, reads=(), writes=(), stream=None, **kw):
        if stream:
            kb = writes[0] if writes else reads[0]
            kb = kb.b if isinstance(kb, T) else kb
            if id(kb) not in self.bufkey:
                self.bufkey[id(kb)] = STREAMS[len(self.bufkey)]
            stream = self.bufkey[id(kb)]
        key = stream if stream else eng
        deps = set()
        for x in reads:
            b = x.b if isinstance(x, T) else x
            if b.lw:
                deps.add(b.lw)
        for x in writes:
            b = x.b if isinstance(x, T) else x
            if b.lw:
                deps.add(b.lw)
            deps.update(b.rd.items())
        if stream is None and (eng == "pe" or not SAME_ENG_SYNC):
            deps = {d for d in deps if d[0] != eng}
        deps = {d for d in deps if d[1] > self.base[d[0]]}
        self.cnt[key] += 1
        n = self.cnt[key]
        tok = (key, n)
        wb = []
        for x in writes:
            b = x.b if isinstance(x, T) else x
            b.lw = tok
            b.rd = {}
            wb.append(b)
        for x in reads:
            b = x.b if isinstance(x, T) else x
            if b not in wb:
                b.rd[key] = n
        self.q[eng].append((meth, kw, deps, tok, stream is not None))
        for d in deps:
            self.sig[d[0]].add(d[1])
        if stream:
            self.last_dma_eng[stream] = eng
        return tok

    def emit_phase(self, name):
        nc = self.nc
        sigl = {k: sorted(v) for k, v in self.sig.items()}

        def sigval(k, n):
            if k in STREAMS:
                return 16 * n
            return self.sigbase[k] + bisect.bisect_right(sigl[k], n)

        with nc.Block() as block:
            deco = {"pe": block.tensor, "act": block.scalar, "dve": block.vector, "pool": block.gpsimd,
                    "sp": block.sync}
            for eng in ENGS:
                q = self.q[eng]
                tail = [s for s in STREAMS if self.last_dma_eng.get(s) == eng and self.cnt[s] > self.base[s]]
                if not q and not tail:
                    continue

                def body(e, q=q, eng=eng, tail=tail):
                    seen = {}
                    for meth, kw, deps, tok, isdma in q:
                        for (k, n) in sorted(deps):
                            v = sigval(k, n)
                            if seen.get(k, 0) < v:
                                e.wait_ge(self.sem[k], v)
                                seen[k] = v
                        ins = getattr(e, meth)(**kw)
                        key, n = tok
                        if isdma:
                            ins.then_inc(self.sem[key], 16)
                        elif n in self.sig[key]:
                            ins.then_inc(self.sem[key], 1)
                    for s in tail:
                        e.wait_ge(self.sem[s], 16 * self.cnt[s])

                deco[eng](body)
        for k in self.keys:
            self.sigbase[k] += len(self.sig[k]) if k not in STREAMS else 0
            self.sig[k] = set()
            self.base[k] = self.cnt[k]
        self.q = {e: [] for e in ENGS}
        self.last_dma_eng = {}
        self.bufkey = {}


class Phase:
    G = 0

    def __init__(self, nc):
        self.nc = nc
        self.es = contextlib.ExitStack()
        self.n = 0

    def __enter__(self):
        self.es.__enter__()
        return self

    def __exit__(self, *a):
        return self.es.__exit__(*a)

    def sb(self, shape, dt, name=None):
        Phase.G += 1
        name = (name or "t") + f"_{Phase.G}"
        return T(self.es.enter_context(self.nc.sbuf_tensor(name, list(shape), dt)), name)

    def ps(self, shape, dt=F32, name=None):
        Phase.G += 1
        name = (name or "p") + f"_{Phase.G}"
        return T(self.es.enter_context(self.nc.psum_tensor(name, list(shape), dt)), name)


class Prog:
    def __init__(self, nc, stop_after=None):
        self.nc = nc
        self.stop_after = stop_after
        dt = nc.dram_tensor
        self.x = dt("x", [L, D], F32, kind="ExternalInput").ap()
        self.ctx = dt("ctx", [LC, D], F32, kind="ExternalInput").ap()
        self.ccol = dt("ccol", [128, 16], F32, kind="ExternalInput").ap()
        self.ada_w = dt("ada_w", [DEPTH, D, 9 * D], F32, kind="ExternalInput").ap()
        self.ada_b_col = dt("ada_b_col", [128, DEPTH * 72], F32, kind="ExternalInput").ap()
        self.npre_col = dt("npre_col", [128, DEPTH * 3 * 8], F32, kind="ExternalInput").ap()
        self.npost_col = dt("npost_col", [128, DEPTH * 3 * 8], F32, kind="ExternalInput").ap()
        self.w_in = dt("ffn_w_in_p", [DEPTH, 2, D, 2 * DFF], F32, kind="ExternalInput").ap()
        self.w_out = dt("ffn_w_out", [DEPTH, 2, DFF, D], F32, kind="ExternalInput").ap()
        self.y = dt("y", [L, D], F32, kind="ExternalOutput").ap()
        self.cs = dt("cs", [LC, D], F32, kind="ExternalOutput" if stop_after else "Internal").ap()
        self.fconst = dt("fconst", [128, 1920], F32, kind="ExternalInput").ap()
        self.fw = dt("fourier_w_out", [2, D, D], F32, kind="ExternalInput").ap()
        self.wfs = dt("wfs", [2, 128, 8 * D], BF16, kind="Internal").ap()
        self.uv = dt("uv", [L, 2 * D], BF16, kind="Internal").ap()
        self.hconst = dt("hconst", [128, 776], F32, kind="ExternalInput").ap()
        self.lbcol = dt("lbcol", [128, 32], F32, kind="ExternalInput").ap()
        self.gncol = dt("gncol", [128, 2], F32, kind="ExternalInput").ap()
        self.hwin = dt("hgrn_w_in", [2, D, 5 * D], F32, kind="ExternalInput").ap()
        self.hwout = dt("hgrn_w_out", [2, D, D], F32, kind="ExternalInput").ap()
        self.wh1s = dt("wh1s", [2, 128, 8 * 5 * D], BF16, kind="Internal").ap()
        self.whos = dt("whos", [2, 128, 8 * D], BF16, kind="Internal").ap()
        NT = 34
        dbgk = "ExternalOutput" if stop_after else "Internal"
        self.hq = [dt(f"hq{d}", [NT, 128, D], BF16, kind=dbgk).ap() for d in range(2)]
        self.hkT = [dt(f"hkT{d}", [NT, 128, D], BF16, kind=dbgk).ap() for d in range(2)]
        self.hk = [dt(f"hk{d}", [NT, 128, D], BF16, kind=dbgk).ap() for d in range(2)]
        self.hv = dt("hv", [NT, 128, D], BF16, kind=dbgk).ap()
        self.hsg = dt("hsg", [NT, 128, D], F32, kind=dbgk).ap()
        self.hdec = dt("hdec", [NT, 128, 64], F32, kind=dbgk).ap()
        self.ho = [dt(f"ho{d}", [NT, 128, D], F32, kind="ExternalOutput" if (stop_after and d == 0) else "Internal").ap() for d in range(2)]
        self.w1s = dt("w1s", [DEPTH, 2, 11, 128, 8 * 512], BF16, kind="Internal").ap()
        self.w2s = dt("w2s", [DEPTH, 2, 128, NFC * D], BF16, kind="Internal").ap()

    def alloc_persistent(self, es):
        nc = self.nc

        def sb(name, shape, dt):
            return T(es.enter_context(nc.sbuf_tensor(name, list(shape), dt)), name)

        self.ident = sb("ident", [128, 128], F32)
        self.identb = sb("identb", [128, 128], BF16)
        self.ones = sb("ones", [128, 128], F32)
        self.epsc = sb("epsc", [128, 1], F32)
        self.mod = sb("mod", [128, DEPTH * 2 * 72], F32)
        self.Acol = sb("Acol", [128, DEPTH * 3 * 2 * 8], F32)
        self.Ccol = sb("Ccol", [128, DEPTH * 3 * 2 * 8], F32)
        self.gpre = sb("gpre", [128, DEPTH * 3 * 8], F32)
        self.gpost = sb("gpost", [128, DEPTH * 3 * 8], F32)

    def modv(self, i, v, m):
        o = ((i * 2 + v) * 9 + m) * 8
        return self.mod[:, o:o + 8]

    def colv(self, tt, i, j, v):
        o = ((i * 3 + j) * 2 + v) * 8
        return tt[:, o:o + 8]

    def phase_init(self, S):
        nc = self.nc
        with Phase(nc) as P:
            iot = P.sb([128, 128], F32)
            pid = P.sb([128, 1], F32)
            S.op("pool", "iota", writes=[iot], out=iot[:], pattern=[[1, 128]], base=0, channel_multiplier=0,
                 allow_small_or_imprecise_dtypes=True)
            S.op("pool", "iota", writes=[pid], out=pid[:], pattern=[[1, 1]], base=0, channel_multiplier=1,
                 allow_small_or_imprecise_dtypes=True)
            S.op("dve", "tensor_scalar", reads=[iot, pid], writes=[self.ident], out=self.ident[:], in0=iot[:],
                 scalar1=pid[:, 0:1], scalar2=None, op0=ALU.is_equal)
            S.op("dve", "tensor_copy", reads=[self.ident], writes=[self.identb], out=self.identb[:],
                 in_=self.ident[:])
            S.op("dve", "memset", writes=[self.ones], ap=self.ones[:], constant=1.0)
            S.op("dve", "memset", writes=[self.epsc], ap=self.epsc[:], constant=EPS)
            S.op("sp", "dma_start", writes=[self.gpre], stream="ld", out=self.gpre[:], in_=self.npre_col)
            S.op("sp", "dma_start", writes=[self.gpost], stream="ld", out=self.gpost[:], in_=self.npost_col)
            S.emit_phase("init")

    def make_jobs(self, layers):
        jobs = []

        def add(unit, src, dst, a, c):
            if a * c == 4096:
                h = a // 2
                jobs.append((unit, src[:, 0:h, :], dst[:, 0:h, :], h, c))
                jobs.append((unit, src[:, h:a, :], dst[:, h:a, :], h, c))
            else:
                jobs.append((unit, src, dst, a, c))

        def ffn(i, f):
            u = ("F", i, f)
            for g in range(11):
                src = self.w_in[i, f, :, g * 512:(g + 1) * 512].rearrange("(kc p) c -> p kc c", p=128)
                add(u, src, self.w1s[i, f, g].rearrange("p (a c) -> p a c", a=8), 8, 512)
            for g in range(11):
                src = self.w_out[i, f, g * 256:(g + 1) * 256, :].rearrange("(fc p) c -> p fc c", p=128)
                add(u, src, self.w2s[i, f, :, g * 2048:(g + 1) * 2048].rearrange("p (a c) -> p a c", a=2), 2, 1024)

        for i in layers:
            ffn(i, 0)
            u = ("M", i)
            if i % 2 == 1:
                for cg in range(10):
                    src = self.hwin[i // 2, :, cg * 512:(cg + 1) * 512].rearrange("(kc p) c -> p kc c", p=128)
                    dst = self.wh1s[i // 2].rearrange("p (kc n) -> p kc n", kc=8)[:, :, cg * 512:(cg + 1) * 512]
                    add(u, src, dst, 8, 512)
                for h in range(2):
                    src = self.hwout[i // 2, :, h * 512:(h + 1) * 512].rearrange("(kc p) c -> p kc c", p=128)
                    dst = self.whos[i // 2].rearrange("p (kc n) -> p kc n", kc=8)[:, :, h * 512:(h + 1) * 512]
                    add(u, src, dst, 8, 512)
            else:
                for h in range(2):
                    src = self.fw[i // 2, :, h * 512:(h + 1) * 512].rearrange("(kc p) c -> p kc c", p=128)
                    dst = self.wfs[i // 2].rearrange("p (kc n) -> p kc n", kc=8)[:, :, h * 512:(h + 1) * 512]
                    add(u, src, dst, 8, 512)
            ffn(i, 1)
        self.jobs = jobs
        self.jnext = 0

    def jobs_pending_for(self, unit):
        last = max([k for k, j in enumerate(self.jobs) if j[0] == unit], default=-1)
        return max(0, last + 1 - self.jnext)

    class BG:
        def __init__(self, pr, S, P):
            self.pr, self.S = pr, S
            self.s32 = [P.sb([128, 2048], F32, f"bg32_{k}") for k in range(2)]
            self.s16 = [P.sb([128, 2048], BF16, f"bg16_{k}") for k in range(2)]
            self.loaded = None
            self.casted = None

        def _load(self, jk):
            unit, src, dst, a, c = self.pr.jobs[jk]
            t = self.s32[jk % 2]
            self.S.op("sp", "dma_start", writes=[t], stream="ld", out=t[:, :].rearrange("p (a c) -> p a c", a=a),
                      in_=src)

        def _finish(self, jk, eng="pool"):
            unit, src, dst, a, c = self.pr.jobs[jk]
            t32, t16 = self.s32[jk % 2], self.s16[jk % 2]
            self.S.op(eng, "tensor_copy", reads=[t32], writes=[t16], out=t16[:], in_=t32[:])
            self.S.op("sp", "dma_start", reads=[t16], stream="st", out=dst,
                      in_=t16[:, :].rearrange("p (a c) -> p a c", a=a))

        def _cast(self, jk, eng):
            t32, t16 = self.s32[jk % 2], self.s16[jk % 2]
            self.S.op(eng, "copy" if eng == "act" else "tensor_copy", reads=[t32], writes=[t16], out=t16[:], in_=t32[:])

        def _store(self, jk):
            unit, src, dst, a, c = self.pr.jobs[jk]
            t16 = self.s16[jk % 2]
            self.S.op("sp", "dma_start", reads=[t16], stream="st", out=dst,
                      in_=t16[:, :].rearrange("p (a c) -> p a c", a=a))

        def step(self, eng="pool"):
            pr = self.pr
            nxt = pr.jnext if pr.jnext < len(pr.jobs) else None
            if nxt is not None:
                self._load(nxt)
                pr.jnext += 1
            if self.casted is not None:
                self._store(self.casted)
            if self.loaded is not None:
                self._cast(self.loaded, eng)
            self.casted = self.loaded
            self.loaded = nxt

        def flush(self, eng="pool"):
            while self.loaded is not None or self.casted is not None:
                if self.casted is not None:
                    self._store(self.casted)
                if self.loaded is not None:
                    self._cast(self.loaded, eng)
                self.casted = self.loaded
                self.loaded = None

    def ensure_unit(self, S, unit):
        n = self.jobs_pending_for(unit)
        if n == 0:
            return
        with Phase(self.nc) as P:
            bg = Prog.BG(self, S, P)
            engs = ("pool", "dve")
            for k in range(n + 1):
                bg.step(engs[k % 2]) if k < n else bg.flush(engs[k % 2])
            S.emit_phase("cvt")

    def phase_prologue(self, S, layers):
        nc = self.nc
        with Phase(nc) as P:
            bg = Prog.BG(self, S, P)
            ncv = self.jobs_pending_for(("F", layers[0], 0))
            cc = P.sb([128, 16], F32)
            s2 = P.sb([128, 16], F32)
            bcol = P.sb([128, DEPTH * 72], F32)
            awb = [P.sb([128, 8 * 512], F32, f"awb{k}") for k in range(3)]
            modrow = P.sb([2, 9 * D], F32, "modrow")
            pr_ = [P.ps([128, 512], F32, f"pr{k}") for k in range(2)]
            pT = [P.ps([128, 512], F32, f"pT{k}") for k in range(2)]
            S.op("sp", "dma_start", writes=[cc], stream="ld", out=cc[:], in_=self.ccol)
            S.op("sp", "dma_start", writes=[bcol], stream="ld", out=bcol[:], in_=self.ada_b_col)
            S.op("act", "activation", reads=[cc], writes=[s2], out=s2[:], in_=cc[:], func=AF.Silu)
            ajobs = [(i, nn) for i in layers for nn in range(18)]

            def aload(k):
                i, nn = ajobs[k]
                S.op("sp", "dma_start", writes=[awb[k % 3]], stream="wt",
                     out=awb[k % 3][:, :].rearrange("p (kc n) -> p kc n", kc=8),
                     in_=self.ada_w[i, :, nn * 512:(nn + 1) * 512].rearrange("(kc p) n -> p kc n", p=128))

            aload(0)
            aload(1)
            cv = 0
            engs = ("pool", "dve")
            for k, (i, nn) in enumerate(ajobs):
                if k + 2 < len(ajobs):
                    aload(k + 2)
                while cv < ncv and cv * len(ajobs) <= k * ncv:
                    bg.step(engs[cv % 2])
                    cv += 1
                w = awb[k % 3]
                ps = pr_[k % 2]
                for kc in range(8):
                    S.op("pe", "matmul", reads=[w, s2], writes=[ps], out=ps[0:2, :], lhsT=s2[:, kc * 2:kc * 2 + 2],
                         rhs=w[:, kc * 512:(kc + 1) * 512], start=(kc == 0), stop=(kc == 7))
                S.op("act", "copy", reads=[ps], writes=[modrow], out=modrow[0:2, nn * 512:(nn + 1) * 512],
                     in_=ps[0:2, :])
                if nn == 17:
                    pt = pT[i % 2]
                    for ch in range(72):
                        S.op("pe", "transpose", reads=[modrow, self.ident], writes=[pt], out=pt[:, ch * 2:ch * 2 + 2],
                             in_=modrow[0:2, ch * 128:(ch + 1) * 128], identity=self.ident[0:2, 0:2])
                    for v in range(2):
                        o = (i * 2 + v) * 72
                        S.op("dve", "tensor_tensor", reads=[pt, bcol], writes=[self.mod],
                             out=self.mod[:, o:o + 72],
                             in0=pt[:, 0:144].rearrange("p (n v) -> p n v", v=2)[:, :, v],
                             in1=bcol[:, i * 72:(i + 1) * 72], op=ALU.add)
            while cv < ncv:
                bg.step(engs[cv % 2])
                cv += 1
            bg.flush()
            for i in layers:
                for j in range(3):
                    w = 1.0 if j == 1 else 0.5
                    g0 = (i * 3 + j) * 8
                    for v in range(2):
                        S.op("dve", "scalar_tensor_tensor", reads=[self.mod, self.gpre], writes=[self.Acol],
                             out=self.colv(self.Acol, i, j, v), in0=self.modv(i, v, 3 * j + 1), scalar=1.0,
                             op0=ALU.add, in1=self.gpre[:, g0:g0 + 8], op1=ALU.mult)
                        S.op("dve", "scalar_tensor_tensor", reads=[self.mod, self.gpost], writes=[self.Ccol],
                             out=self.colv(self.Ccol, i, j, v), in0=self.modv(i, v, 3 * j + 2), scalar=w,
                             op0=ALU.mult, in1=self.gpost[:, g0:g0 + 8], op1=ALU.mult)
            if self.stop_after is not None:
                dbg = self.nc.dram_tensor("dbg", [128, DEPTH * 2 * 72], F32, kind="ExternalOutput").ap()
                S.op("sp", "dma_start", reads=[self.mod], stream="st", out=dbg, in_=self.mod[:])
            S.emit_phase("prologue")

    def make_bc(self, S, P, col_ap, col_reads, out_t, scratch, ps):
        for kc in range(8):
            S.op("dve", "tensor_scalar", reads=[self.ident] + col_reads, writes=[scratch],
                 out=scratch[:, kc * 128:(kc + 1) * 128], in0=self.ident[:], scalar1=col_ap[:, kc:kc + 1],
                 scalar2=None, op0=ALU.mult)
        for h in range(2):
            S.op("pe", "matmul", reads=[self.ones, scratch], writes=[ps], out=ps[:, h * 512:(h + 1) * 512],
                 lhsT=self.ones[:], rhs=scratch[:, h * 512:(h + 1) * 512], start=True, stop=True)
        S.op("act", "copy", reads=[ps], writes=[out_t], out=out_t[:], in_=ps[:])

    def rstd_from_ss(self, S, st):
        S.op("act", "activation", reads=[st, self.epsc], writes=[st], out=st[:, 1:2], in_=st[:, 0:1], func=AF.Sqrt,
             scale=1.0 / D, bias=self.epsc[:, 0:1])
        S.op("dve", "reciprocal", reads=[st], writes=[st], out=st[:, 2:3], in_=st[:, 1:2])

    def phase_ffn(self, S, i, f, do_ctx, x_src, c_src=None):
        nc = self.nc
        j = 0 if f == 0 else 2
        with Phase(nc) as P:
            w2 = P.sb([128, NFC * D], BF16, "w2")
            NW1 = 4
            w1 = [P.sb([128, 8 * 512], BF16, f"w1_{k}") for k in range(NW1)]
            hTs = [P.sb([128, 8 * 512], BF16, f"hT{k}") for k in range(2)]
            actT = P.sb([128, NFC * 512], BF16, "actT")
            xa = [P.sb([128, D], F32, f"xa{k}") for k in range(2)]
            xn = [P.sb([128, D], F32, f"xn{k}") for k in range(2)]
            xb = [P.sb([128, D], F32, f"xb{k}") for k in range(2)]
            tmp = [P.sb([128, D], F32, f"tmp{k}") for k in range(2)]
            junk = P.sb([128, D], F32, "junk")
            sg = [P.sb([128, 512], F32, f"sg{k}") for k in range(2)]
            Cbc = [P.sb([128, D], F32, f"Cbc{v}") for v in range(2)]
            sta = [P.sb([128, 4], F32, f"sta{k}") for k in range(4)]
            stb = [P.sb([128, 4], F32, f"stb{k}") for k in range(4)]
            GU = [[P.ps([128, 512], F32, f"gu{a}{b}") for b in range(2)] for a in range(2)]
            YT = [P.ps([128, D], F32, f"yt{k}") for k in range(2)]

            bg = Prog.BG(self, S, P)
            nv = 2 if do_ctx else 1
            for v in range(nv):
                self.make_bc(S, P, self.colv(self.Ccol, i, j, v), [self.Ccol], Cbc[v], junk, YT[v])

            blocks = [(x_src, self.y, b * 512, 4, 0) for b in range(8)]
            if do_ctx:
                csrc = c_src if c_src is not None else self.cs
                blocks.append((csrc, self.cs, 0, 2, 1))
            nb = len(blocks)
            wjobs = [(bi, g) for bi in range(nb) for g in range(11)]
            wstate = {"next": 0}

            def w1_prefetch(upto):
                while wstate["next"] <= min(upto, len(wjobs) - 1):
                    k = wstate["next"]
                    S.op("sp", "dma_start", writes=[w1[k % NW1]], stream="wt", out=w1[k % NW1][:],
                         in_=self.w1s[i, f, wjobs[k][1]])
                    wstate["next"] += 1

            cnt = {"a": 0, "y": 0, "gu": 0, "st": 0}

            s1state = {}

            def stage1(bi, tiles=None, part="AB"):
                src, dst, tok0, nt, v = blocks[bi]
                hT = hTs[bi % 2]
                A = self.colv(self.Acol, i, j, v)
                for t in (range(nt) if tiles is None else tiles):
                    if t >= nt:
                        continue
                    if "A" in part:
                        k = cnt["a"]
                        cnt["a"] += 1
                        xt, xnt, st = xa[k % 2], xn[k % 2], sta[k % 4]
                        s1state[(bi, t)] = xnt
                        r0 = tok0 + t * 128
                        S.op("sp", "dma_start", writes=[xt], stream="ld", out=xt[:], in_=src[r0:r0 + 128, :])
                        S.op("act", "activation", reads=[xt], writes=[junk, st], out=junk[:], in_=xt[:],
                             func=AF.Square, accum_out=st[:, 0:1])
                        self.rstd_from_ss(S, st)
                        S.op("act", "activation", reads=[xt, st], writes=[xnt], out=xnt[:], in_=xt[:],
                             func=AF.Copy, scale=st[:, 2:3])
                    if "B" in part:
                        xnt = s1state.pop((bi, t))
                        pt = YT[cnt["y"] % 2]
                        cnt["y"] += 1
                        for kc in range(8):
                            S.op("pe", "transpose", reads=[xnt, self.ident], writes=[pt],
                                 out=pt[:, kc * 128:(kc + 1) * 128], in_=xnt[:, kc * 128:(kc + 1) * 128],
                                 identity=self.ident[:])
                        for kc in range(8):
                            S.op("dve", "tensor_scalar", reads=[pt, self.Acol, self.mod], writes=[hT],
                                 out=hT[:, kc * 512 + t * 128: kc * 512 + (t + 1) * 128],
                                 in0=pt[:, kc * 128:(kc + 1) * 128], scalar1=A[:, kc:kc + 1],
                                 scalar2=self.modv(i, v, 3 * j)[:, kc:kc + 1], op0=ALU.mult, op1=ALU.add)

            def stage2(bi):
                src, dst, tok0, nt, v = blocks[bi]
                hT = hTs[bi % 2]
                TT = nt * 128
                for fc in range(NFC):
                    if bi + 1 < nb and fc in (0, 5, 10, 15):
                        stage1(bi + 1, [fc // 5], "A")
                    if bi + 1 < nb and fc in (4, 9, 14, 19):
                        stage1(bi + 1, [(fc - 4) // 5], "B")
                    k = bi * 11 + fc // 2
                    if fc % 2 == 0:
                        w1_prefetch(k + NW1 - 1)
                    if fc % 2 == 1:
                        bg.step("act")
                    w = w1[k % NW1]
                    fcl = fc % 2
                    gk = cnt["gu"] % 2
                    cnt["gu"] += 1
                    for half in range(2):
                        ps = GU[gk][half]
                        for kc in range(8):
                            o = kc * 512 + (fcl * 2 + half) * 128
                            S.op("pe", "matmul", reads=[w, hT], writes=[ps], out=ps[:, 0:TT], lhsT=w[:, o:o + 128],
                                 rhs=hT[:, kc * 512: kc * 512 + TT], start=(kc == 0), stop=(kc == 7))
                    S.op("act", "activation", reads=[GU[gk][0]], writes=[sg[gk]], out=sg[gk][:, 0:TT],
                         in_=GU[gk][0][:, 0:TT], func=AF.Silu)
                    S.op("dve", "tensor_tensor", reads=[sg[gk], GU[gk][1]], writes=[actT],
                         out=actT[:, fc * 512: fc * 512 + TT], in0=sg[gk][:, 0:TT], in1=GU[gk][1][:, 0:TT],
                         op=ALU.mult)

            def stage3(bi):
                src, dst, tok0, nt, v = blocks[bi]
                for t in range(nt):
                    k = cnt["st"]
                    cnt["st"] += 1
                    xt, st, yp, tm = xb[k % 2], stb[k % 4], YT[cnt["y"] % 2], tmp[k % 2]
                    cnt["y"] += 1
                    r0 = tok0 + t * 128
                    S.op("sp", "dma_start", writes=[xt], stream="ld", out=xt[:], in_=src[r0:r0 + 128, :])
                    for dh in range(2):
                        for fc in range(NFC):
                            S.op("pe", "matmul", reads=[actT, w2], writes=[yp], out=yp[:, dh * 512:(dh + 1) * 512],
                                 lhsT=actT[:, fc * 512 + t * 128: fc * 512 + (t + 1) * 128],
                                 rhs=w2[:, fc * D + dh * 512: fc * D + (dh + 1) * 512], start=(fc == 0),
                                 stop=(fc == NFC - 1))
                    S.op("act", "activation", reads=[yp], writes=[junk, st], out=junk[:], in_=yp[:],
                         func=AF.Square, accum_out=st[:, 0:1])
                    self.rstd_from_ss(S, st)
                    S.op("dve", "scalar_tensor_tensor", reads=[yp, st, Cbc[v]], writes=[tm], out=tm[:], in0=yp[:],
                         scalar=st[:, 2:3], op0=ALU.mult, in1=Cbc[v][:], op1=ALU.mult)
                    S.op("dve", "tensor_tensor", reads=[tm, xt], writes=[tm], out=tm[:], in0=tm[:], in1=xt[:],
                         op=ALU.add)
                    S.op("sp", "dma_start", reads=[tm], stream="st", out=dst[r0:r0 + 128, :], in_=tm[:])

            w1_prefetch(NW1 - 2)
            S.op("sp", "dma_start", writes=[w2], stream="wt", out=w2[:], in_=self.w2s[i, f])
            stage1(0)
            for bi in range(nb):
                stage2(bi)
                stage3(bi)
            bg.flush("act")
            S.emit_phase(f"ffn{i}{f}")


    def load_fconst(self, S, P):
        fc32 = P.sb([128, 1920], F32, "fc32")
        fcb = P.sb([128, 1920], BF16, "fcb")
        S.op("sp", "dma_start", writes=[fc32], stream="ld", out=fc32[:], in_=self.fconst)
        S.op("dve", "tensor_copy", reads=[fc32], writes=[fcb], out=fcb[:], in_=fc32[:])
        return fcb

    def pre_tokmajor(self, S, src_rows, xt, st, tm, hb, junk, Abc, Bbc):
        for (p0, npart, ap) in (src_rows or []):
            S.op("sp", "dma_start", writes=[xt], stream="ld", out=xt[p0:p0 + npart, :], in_=ap)
        S.op("act", "activation", reads=[xt], writes=[junk, st], out=junk[:], in_=xt[:], func=AF.Square,
             accum_out=st[:, 0:1])
        self.rstd_from_ss(S, st)
        S.op("dve", "scalar_tensor_tensor", reads=[xt, st, Abc], writes=[tm], out=tm[:], in0=xt[:],
             scalar=st[:, 2:3], op0=ALU.mult, in1=Abc[:], op1=ALU.mult)
        S.op("dve", "tensor_tensor", reads=[tm, Bbc], writes=[hb], out=hb[:], in0=tm[:], in1=Bbc[:], op=ALU.add)

    def phase_fourier1(self, S, i):
        nc = self.nc
        j = 1
        with Phase(nc) as P:
            fcb = self.load_fconst(S, P)
            Abc = P.sb([128, D], F32, "Abc")
            Bbc = P.sb([128, D], F32, "Bbc")
            junk = P.sb([128, D], F32, "junk")
            xa = [P.sb([128, D], F32, f"xa{k}") for k in range(3)]
            tm = [P.sb([128, D], F32, f"tm{k}") for k in range(2)]
            hb = [P.sb([128, D], BF16, f"hb{k}") for k in range(2)]
            uvs = [P.sb([128, 2 * D], BF16, f"uvs{k}") for k in range(2)]
            st = [P.sb([128, 4], F32, f"st{k}") for k in range(4)]
            PU = [P.ps([128, D], F32, f"pu{k}") for k in range(2)]
            PV = [P.ps([128, D], F32, f"pv{k}") for k in range(2)]
            self.make_bc(S, P, self.colv(self.Acol, i, j, 0), [self.Acol], Abc, junk, PU[0])
            self.make_bc(S, P, self.modv(i, 0, 3 * j), [self.mod], Bbc, junk, PU[1])
            def f1load(t):
                S.op("sp", "dma_start", writes=[xa[t % 3]], stream="ld", out=xa[t % 3][:],
                     in_=self.y[t * 128:(t + 1) * 128, :])

            f1load(0)
            f1load(1)
            for t in range(32):
                k = t % 2
                if t + 2 < 32:
                    f1load(t + 2)
                self.pre_tokmajor(S, None, xa[t % 3], st[t % 4], tm[k], hb[k], junk, Abc, Bbc)
                for h in range(2):
                    S.op("pe", "matmul", reads=[fcb, hb[k]], writes=[PU[k]], out=PU[k][:, h * 512:(h + 1) * 512],
                         lhsT=fcb[:, 0:128], rhs=hb[k][:, h * 512:(h + 1) * 512], start=True, stop=True)
                for h in range(2):
                    S.op("pe", "matmul", reads=[fcb, hb[k]], writes=[PV[k]], out=PV[k][:, h * 512:(h + 1) * 512],
                         lhsT=fcb[:, 128:256], rhs=hb[k][:, h * 512:(h + 1) * 512], start=True, stop=True)
                S.op("act", "copy", reads=[PU[k]], writes=[uvs[k]], out=uvs[k][:, 0:D], in_=PU[k][:])
                S.op("dve", "tensor_copy", reads=[PV[k]], writes=[uvs[k]], out=uvs[k][:, D:2 * D], in_=PV[k][:])
                S.op("sp", "dma_start", reads=[uvs[k]], stream="st", out=self.uv[t * 128:(t + 1) * 128, :],
                     in_=uvs[k][:])
            S.emit_phase(f"four1_{i}")

    def phase_fourier2(self, S, i, do_ctx):
        nc = self.nc
        j = 1
        jm = i // 2
        with Phase(nc) as P:
            fcb = self.load_fconst(S, P)
            BDc, BDs, nBDs = fcb[:, 0:128], fcb[:, 128:256], fcb[:, 256:384]

            def Ck(kc, kk):
                return fcb[:, 384 + kc * 256 + kk * 128: 384 + kc * 256 + (kk + 1) * 128]

            def Sk(kc, kk):
                return fcb[:, 896 + kc * 256 + kk * 128: 896 + kc * 256 + (kk + 1) * 128]

            def nSk(kc, kk):
                return fcb[:, 1408 + kc * 256 + kk * 128: 1408 + kc * 256 + (kk + 1) * 128]

            wf = P.sb([128, 8 * D], BF16, "wf")
            S.op("sp", "dma_start", writes=[wf], stream="wt", out=wf[:], in_=self.wfs[jm])
            junk = P.sb([128, D], F32, "junk")
            Cbc = [P.sb([128, D], F32, f"Cbc{v}") for v in range(2)]
            uvt = [P.sb([128, 2 * D], BF16, f"uvt{k}") for k in range(3)]
            abT = [P.sb([128, 2 * D], BF16, f"abT{k}") for k in range(2)]
            yTs = [P.sb([128, D], BF16, f"yTs{k}") for k in range(2)]
            xb = [P.sb([128, D], F32, f"xb{k}") for k in range(3)]
            tm = [P.sb([128, D], F32, f"tm{k}") for k in range(2)]
            st = [P.sb([128, 4], F32, f"st{k}") for k in range(4)]
            PA = P.ps([128, D], F32, "pa")
            PB = P.ps([128, D], F32, "pb")
            PY = P.ps([128, D], F32, "py")
            PO = P.ps([128, D], F32, "po")
            nv = 2 if do_ctx else 1
            for v in range(nv):
                self.make_bc(S, P, self.colv(self.Ccol, i, j, v), [self.Ccol], Cbc[v], junk, PO)
            cnt = {"k": 0}

            def tile_proc(termsA, termsB, term_reads, res_rows, v, preloaded=False, xbt=None):
                k = cnt["k"] % 2
                cnt["k"] += 1
                for (terms, PX) in ((termsA, PA), (termsB, PB)):
                    for dc in range(8):
                        for ti, (lh, rh) in enumerate(terms):
                            S.op("pe", "matmul", reads=[fcb] + term_reads, writes=[PX],
                                 out=PX[:, dc * 128:(dc + 1) * 128], lhsT=lh(dc), rhs=rh, start=(ti == 0),
                                 stop=(ti == len(terms) - 1))
                S.op("act", "copy", reads=[PA], writes=[abT[k]], out=abT[k][:, 0:D], in_=PA[:])
                S.op("dve", "tensor_copy", reads=[PB], writes=[abT[k]], out=abT[k][:, D:2 * D], in_=PB[:])
                for g in range(4):
                    for kk in range(2):
                        oc = (g * 2 + kk) * 128
                        n = 0
                        for kc in range(2):
                            ic = (g * 2 + kc) * 128
                            S.op("pe", "matmul", reads=[fcb, abT[k]], writes=[PY], out=PY[:, oc:oc + 128],
                                 lhsT=Ck(kc, kk), rhs=abT[k][:, ic:ic + 128], start=(n == 0), stop=False)
                            n += 1
                            S.op("pe", "matmul", reads=[fcb, abT[k]], writes=[PY], out=PY[:, oc:oc + 128],
                                 lhsT=nSk(kc, kk), rhs=abT[k][:, D + ic:D + ic + 128], start=False, stop=(n == 3))
                            n += 1
                S.op("act", "copy", reads=[PY], writes=[yTs[k]], out=yTs[k][:], in_=PY[:])
                for dh in range(2):
                    for kc in range(8):
                        S.op("pe", "matmul", reads=[yTs[k], wf], writes=[PO], out=PO[:, dh * 512:(dh + 1) * 512],
                             lhsT=yTs[k][:, kc * 128:(kc + 1) * 128],
                             rhs=wf[:, kc * D + dh * 512: kc * D + (dh + 1) * 512], start=(kc == 0), stop=(kc == 7))
                xt, stt, tmm = (xbt if xbt is not None else xb[k]), st[cnt["k"] % 4], tm[k]
                if not preloaded:
                    for (p0, npart, ap) in res_rows:
                        S.op("sp", "dma_start", writes=[xt], stream="ld", out=xt[p0:p0 + npart, :], in_=ap)
                S.op("act", "activation", reads=[PO], writes=[junk, stt], out=junk[:], in_=PO[:], func=AF.Square,
                     accum_out=stt[:, 0:1])
                self.rstd_from_ss(S, stt)
                S.op("dve", "scalar_tensor_tensor", reads=[PO, stt, Cbc[v]], writes=[tmm], out=tmm[:], in0=PO[:],
                     scalar=stt[:, 2:3], op0=ALU.mult, in1=Cbc[v][:], op1=ALU.mult)
                S.op("dve", "tensor_tensor", reads=[tmm, xt], writes=[tmm], out=tmm[:], in0=tmm[:], in1=xt[:],
                     op=ALU.add)
                for (p0, npart, ap) in res_rows:
                    S.op("sp", "dma_start", reads=[tmm], stream="st", out=ap, in_=tmm[p0:p0 + npart, :])

            if do_ctx:
                Abc = P.sb([128, D], F32, "Abc")
                Bbc = P.sb([128, D], F32, "Bbc")
                self.make_bc(S, P, self.colv(self.Acol, i, j, 1), [self.Acol], Abc, junk, PO)
                self.make_bc(S, P, self.modv(i, 1, 3 * j), [self.mod], Bbc, junk, PO)
                hc = [P.sb([128, D], BF16, f"hc{k}") for k in range(2)]
                xa = P.sb([128, D], F32, "xa")
                for lt in range(2):
                    self.pre_tokmajor(S, [(0, 128, self.cs[lt * 128:(lt + 1) * 128, :])], xa, st[lt], tm[lt], hc[lt],
                                      junk, Abc, Bbc)
                for hh in range(2):
                    tA = [((lambda dc, lt=lt: hc[lt][:, dc * 128:(dc + 1) * 128]), Ck(lt, hh)) for lt in range(2)]
                    tB = [((lambda dc, lt=lt: hc[lt][:, dc * 128:(dc + 1) * 128]), Sk(lt, hh)) for lt in range(2)]
                    tile_proc(tA, tB, [hc[0], hc[1]], [(0, 128, self.cs[hh * 128:(hh + 1) * 128, :])], 1)

            yv = self.y.rearrange("(r c) f -> c r f", c=64)
            uvv = self.uv.rearrange("(r c) f -> c r f", c=64)
            kbase = cnt["k"]

            def f2load(u):
                ut = uvt[u % 3]
                xt = xb[u % 3]
                for cb in range(2):
                    S.op("sp", "dma_start", writes=[ut], stream="ld", out=ut[cb * 64:(cb + 1) * 64, :],
                         in_=uvv[2 * u + cb])
                for cb in range(2):
                    S.op("sp", "dma_start", writes=[xt], stream="ld", out=xt[cb * 64:(cb + 1) * 64, :],
                         in_=yv[2 * u + cb])

            f2load(0)
            f2load(1)
            for u in range(32):
                ut = uvt[u % 3]
                if u + 2 < 32:
                    f2load(u + 2)
                U = lambda dc, ut=ut: ut[:, dc * 128:(dc + 1) * 128]
                V = lambda dc, ut=ut: ut[:, D + dc * 128: D + (dc + 1) * 128]
                tile_proc([(U, BDc), (V, nBDs)], [(V, BDc), (U, BDs)], [ut],
                          [(cb * 64, 64, yv[2 * u + cb]) for cb in range(2)], 0, preloaded=True, xbt=xb[u % 3])
            S.emit_phase(f"four2_{i}")

    def tile_rows(self, tt):
        return self.cs[tt * 128:(tt + 1) * 128, :] if tt < 2 else self.y[(tt - 2) * 128:(tt - 1) * 128, :]

    def phase_h1(self, S, i):
        nc = self.nc
        j = 1
        jm = i // 2
        with Phase(nc) as P:
            hc = P.sb([128, 776], F32, "hc")
            S.op("sp", "dma_start", writes=[hc], stream="ld", out=hc[:], in_=self.hconst)
            hcb = P.sb([128, 264], BF16, "hcb")
            S.op("dve", "tensor_copy", reads=[hc], writes=[hcb], out=hcb[:], in_=hc[:, 0:264])
            Dm = [hcb[:, 0:128], hcb[:, 132:260]]
            DmD = [hcb[:, 0:132], hcb[:, 132:264]]
            wh = P.sb([128, 8 * 5 * D], BF16, "wh")
            for kc in range(8):
                S.op("sp", "dma_start", writes=[wh], stream="wt", out=wh[:, kc * 5120:(kc + 1) * 5120],
                     in_=self.wh1s[jm][:, kc * 5120:(kc + 1) * 5120])
            junk = P.sb([128, D], F32, "junk")
            PT = P.ps([128, D], F32, "PT")
            banks = [P.ps([128, 512], F32, f"bk{k}") for k in range(6)]
            bstate = {"n": 0}

            def nb():
                bstate["n"] += 1
                return banks[bstate["n"] % 6]

            lb_bc = oml_bc = None
            if jm == 1:
                lbc = P.sb([128, 32], F32, "lbc")
                lbd = P.sb([128, 16], F32, "lbd")
                oml = P.sb([128, 16], F32, "oml")
                S.op("sp", "dma_start", writes=[lbc], stream="ld", out=lbc[:], in_=self.lbcol)
                lv = lbc[:, :].rearrange("p (d l k) -> p d l k", d=2, l=2)
                S.op("dve", "tensor_tensor", reads=[lbc], writes=[lbd], out=lbd[:, :].rearrange("p (d k) -> p d k", d=2),
                     in0=lv[:, :, 1, :], in1=lv[:, :, 0, :], op=ALU.subtract)
                S.op("act", "activation", reads=[lbd], writes=[lbd], out=lbd[:], in_=lbd[:], func=AF.Sigmoid)
                S.op("dve", "tensor_scalar", reads=[lbd], writes=[oml], out=oml[:], in0=lbd[:], scalar1=-1.0,
                     scalar2=1.0, op0=ALU.mult, op1=ALU.add)
                lb_bc = [P.sb([128, D], F32, f"lbbc{d}") for d in range(2)]
                oml_bc = [P.sb([128, D], F32, f"omlbc{d}") for d in range(2)]
                for d in range(2):
                    self.make_bc(S, P, lbd[:, d * 8:(d + 1) * 8], [lbd], lb_bc[d], junk, PT)
                    self.make_bc(S, P, oml[:, d * 8:(d + 1) * 8], [oml], oml_bc[d], junk, PT)
            xa = [P.sb([128, D], F32, f"xa{k}") for k in range(2)]
            xn = [P.sb([128, D], F32, "xn0")] * 2
            sta = [P.sb([128, 4], F32, f"sta{k}") for k in range(4)]
            hT = [P.sb([128, D], BF16, f"hT{k}") for k in range(2)]
            qT32 = [P.sb([128, D], F32, f"qT32{k}") for k in range(2)]
            vs = [P.sb([128, D], BF16, f"vs{k}") for k in range(2)]
            sgs = [P.sb([128, D], F32, f"sgs{k}") for k in range(2)]
            ft = [[P.sb([128, D], F32, f"ft{k}{d}") for d in range(2)] for k in range(2)]
            lt = [P.sb([128, D], F32, "lt0")] * 2
            lhi = [P.sb([128, D], BF16, f"lhi{d}") for d in range(2)]
            llo = [P.sb([128, D], BF16, f"llo{d}") for d in range(2)]
            ent = [P.sb([128, D], F32, f"ent{d}") for d in range(2)]
            eqx = [P.sb([128, 8 * 132], F32, f"eqx{d}") for d in range(2)]
            ktb = [P.sb([128, D], BF16, f"ktb{d}") for d in range(2)]
            kTb = [P.sb([128, D], BF16, f"kTb{d}") for d in range(2)]
            qtb = [P.sb([128, D], BF16, f"qtb{d}") for d in range(2)]
            dect = [P.sb([128, 64], F32, f"dect{k}") for k in range(2)]

            def stageA(tt, part="AB"):
                v = 1 if tt < 2 else 0
                A = self.colv(self.Acol, i, j, v)
                k = tt % 2
                xt, xnt, st = xa[k], xn[k], sta[tt % 4]
                if "A" in part:
                    S.op("sp", "dma_start", writes=[xt], stream="ld", out=xt[:], in_=self.tile_rows(tt))
                    S.op("act", "activation", reads=[xt], writes=[junk, st], out=junk[:], in_=xt[:], func=AF.Square,
                         accum_out=st[:, 0:1])
                    self.rstd_from_ss(S, st)
                    S.op("act", "activation", reads=[xt, st], writes=[xnt], out=xnt[:], in_=xt[:], func=AF.Copy,
                         scale=st[:, 2:3])
                if "B" in part:
                    for kc in range(8):
                        S.op("pe", "transpose", reads=[xnt, self.ident], writes=[PT], out=PT[:, kc * 128:(kc + 1) * 128],
                             in_=xnt[:, kc * 128:(kc + 1) * 128], identity=self.ident[:])
                    for kc in range(8):
                        S.op("dve", "tensor_scalar", reads=[PT, self.Acol, self.mod], writes=[hT[k]],
                             out=hT[k][:, kc * 128:(kc + 1) * 128], in0=PT[:, kc * 128:(kc + 1) * 128],
                             scalar1=A[:, kc:kc + 1], scalar2=self.modv(i, v, 3 * j)[:, kc:kc + 1], op0=ALU.mult,
                             op1=ALU.add)

            def projq(tt, p):
                k = tt % 2
                bk = nb()
                for hh in range(4):
                    h = p * 4 + hh
                    for kc in range(8):
                        S.op("pe", "matmul", reads=[wh, hT[k]], writes=[bk], out=bk[:, hh * 128:(hh + 1) * 128],
                             lhsT=wh[:, kc * 5120 + h * 128: kc * 5120 + (h + 1) * 128],
                             rhs=hT[k][:, kc * 128:(kc + 1) * 128], start=(kc == 0), stop=(kc == 7))
                S.op("act", "activation", reads=[bk], writes=[qT32[k]], out=qT32[k][:, p * 512:(p + 1) * 512],
                     in_=bk[:], func=AF.Silu)

            def projtok(tt, c0, p, func, dst, iscopy=False):
                k = tt % 2
                bk = nb()
                for kc in range(8):
                    o = kc * 5120 + c0 + p * 512
                    S.op("pe", "matmul", reads=[wh, hT[k]], writes=[bk], out=bk[:], lhsT=hT[k][:, kc * 128:(kc + 1) * 128],
                         rhs=wh[:, o:o + 512], start=(kc == 0), stop=(kc == 7))
                if iscopy:
                    S.op("act", "copy", reads=[bk], writes=[dst], out=dst[:, p * 512:(p + 1) * 512], in_=bk[:])
                else:
                    S.op("act", "activation", reads=[bk], writes=[dst], out=dst[:, p * 512:(p + 1) * 512], in_=bk[:],
                         func=func)

            def stageB_parts(tt):
                k = tt % 2
                parts = []
                parts.append(lambda: [projq(tt, p) for p in range(2)])

                def vg():
                    for p in range(2):
                        projtok(tt, 1024, p, None, vs[k], iscopy=True)
                    S.op("sp", "dma_start", reads=[vs[k]], stream="st", out=self.hv[tt], in_=vs[k][:])
                    for p in range(2):
                        projtok(tt, 4096, p, AF.Silu, sgs[k])
                    S.op("sp", "dma_start", reads=[sgs[k]], stream="st", out=self.hsg[tt], in_=sgs[k][:])
                parts.append(vg)

                def zz():
                    for d in range(2):
                        for p in range(2):
                            projtok(tt, 2048 + d * 1024, p, AF.Sigmoid, ft[k][d])
                        if jm == 1:
                            f = ft[k][d]
                            S.op("dve", "tensor_tensor", reads=[f, oml_bc[d]], writes=[f], out=f[:], in0=f[:],
                                 in1=oml_bc[d][:], op=ALU.mult)
                            S.op("dve", "tensor_tensor", reads=[f, lb_bc[d]], writes=[f], out=f[:], in0=f[:],
                                 in1=lb_bc[d][:], op=ALU.add)
                parts.append(zz)
                return parts

            def stageC_parts(tt):
                k = tt % 2
                parts = []

                def c1():
                    for d in range(2):
                        S.op("act", "activation", reads=[ft[k][d]], writes=[lt[d]], out=lt[d][:], in_=ft[k][d][:],
                             func=AF.Ln)
                        S.op("dve", "tensor_copy", reads=[lt[d]], writes=[lhi[d]], out=lhi[d][:], in_=lt[d][:])
                        S.op("dve", "tensor_tensor", reads=[lt[d], lhi[d]], writes=[llo[d]], out=llo[d][:], in0=lt[d][:],
                             in1=lhi[d][:], op=ALU.subtract)
                    kt = ft[k]
                    for d in range(2):
                        S.op("dve", "tensor_scalar", reads=[ft[k][d]], writes=[kt[d]], out=kt[d][:], in0=ft[k][d][:],
                             scalar1=-1.0, scalar2=1.0, op0=ALU.mult, op1=ALU.add)
                    for d in range(2):
                        for p in range(2):
                            bk = nb()
                            S.op("pe", "matmul", reads=[hcb, lhi[d]], writes=[bk], out=bk[:], lhsT=Dm[d],
                                 rhs=lhi[d][:, p * 512:(p + 1) * 512], start=True, stop=False)
                            S.op("pe", "matmul", reads=[hcb, llo[d]], writes=[bk], out=bk[:], lhsT=Dm[d],
                                 rhs=llo[d][:, p * 512:(p + 1) * 512], start=False, stop=True)
                            S.op("act", "activation", reads=[bk], writes=[ent[d]], out=ent[d][:, p * 512:(p + 1) * 512],
                                 in_=bk[:], func=AF.Exp, scale=-1.0)
                parts.append(c1)

                def c2():
                    kt = ft[k]
                    for d in range(2):
                        S.op("dve", "tensor_tensor", reads=[kt[d], ent[d]], writes=[ktb[d]], out=ktb[d][:], in0=kt[d][:],
                             in1=ent[d][:], op=ALU.mult)
                        S.op("sp", "dma_start", reads=[ktb[d]], stream="st", out=self.hk[d][tt], in_=ktb[d][:])
                    for d in range(2):
                        for (h0, nh) in ((0, 3), (3, 3), (6, 2)):
                            bk = nb()
                            for hh in range(nh):
                                h = h0 + hh
                                S.op("pe", "matmul", reads=[hcb, lhi[d]], writes=[bk], out=bk[:, hh * 132:(hh + 1) * 132],
                                     lhsT=lhi[d][:, h * 128:(h + 1) * 128], rhs=DmD[d], start=True, stop=False)
                                S.op("pe", "matmul", reads=[hcb, llo[d]], writes=[bk], out=bk[:, hh * 132:(hh + 1) * 132],
                                     lhsT=llo[d][:, h * 128:(h + 1) * 128], rhs=DmD[d], start=False, stop=True)
                            S.op("act", "activation", reads=[bk], writes=[eqx[d]],
                                 out=eqx[d][:, h0 * 132:(h0 + nh) * 132], in_=bk[:, 0:nh * 132], func=AF.Exp)
                    for d in range(2):
                        S.op("dve", "tensor_copy", reads=[eqx[d]], writes=[dect[k]],
                             out=dect[k][:, d * 32:(d + 1) * 32].rearrange("p (h e) -> p h e", h=8),
                             in_=eqx[d][:, :].rearrange("p (h e) -> p h e", h=8)[:, :, 128:132])
                    S.op("sp", "dma_start", reads=[dect[k]], stream="st", out=self.hdec[tt], in_=dect[k][:])
                parts.append(c2)

                def c3():
                    for d in range(2):
                        S.op("dve", "tensor_tensor", reads=[qT32[k], eqx[d]], writes=[qtb[d]],
                             out=qtb[d][:, :].rearrange("p (h t) -> p h t", h=8),
                             in0=qT32[k][:, :].rearrange("p (h t) -> p h t", h=8),
                             in1=eqx[d][:, :].rearrange("p (h e) -> p h e", h=8)[:, :, 0:128], op=ALU.mult)
                        S.op("sp", "dma_start", reads=[qtb[d]], stream="st", out=self.hq[d][tt], in_=qtb[d][:])
                    for d in range(2):
                        bk = nb()
                        pk = bk[:, :].bitcast(BF16)
                        for h in range(8):
                            S.op("pe", "transpose", reads=[ktb[d], self.identb], writes=[bk],
                                 out=pk[:, h * 128:(h + 1) * 128], in_=ktb[d][:, h * 128:(h + 1) * 128],
                                 identity=self.identb[:])
                        S.op("dve", "tensor_copy", reads=[bk], writes=[kTb[d]], out=kTb[d][:], in_=pk)
                        S.op("sp", "dma_start", reads=[kTb[d]], stream="st", out=self.hkT[d][tt], in_=kTb[d][:])
                parts.append(c3)
                return parts

            NTL = 34
            stageA(0)
            for f_ in stageB_parts(0):
                f_()
            stageA(1, "A")
            for tt in range(NTL):
                if tt + 1 < NTL:
                    stageA(tt + 1, "B")
                cp = stageC_parts(tt)
                bp = stageB_parts(tt + 1) if tt + 1 < NTL else []
                for n in range(3):
                    cp[n]()
                    if n == 1 and tt + 2 < NTL:
                        stageA(tt + 2, "A")
                    if n < len(bp):
                        bp[n]()
            S.emit_phase(f"h1_{i}")

    def phase_hscan(self, S, i, dirs=(0, 1)):
        nc = self.nc
        with Phase(nc) as P:
            hc = P.sb([128, 776], F32, "hc")
            S.op("sp", "dma_start", writes=[hc], stream="ld", out=hc[:], in_=self.hconst)
            gens = [self.hscan_gen(S, P, i, d, hc) for d in dirs]
            while gens:
                for g_ in list(gens):
                    try:
                        next(g_)
                    except StopIteration:
                        gens.remove(g_)
            S.emit_phase(f"hscan{i}")

    def hscan_gen(self, S, P, i, d, hc):
        Um = hc[:, 264 + d * 256: 264 + d * 256 + 128]
        Lm = hc[:, 392 + d * 256: 392 + d * 256 + 128]
        PAT = P.ps([128, 512], F32, f"PAT{d}")
        PO = P.ps([128, 512], F32, f"PO{d}")
        PI = P.ps([128, 512], F32, f"PI{d}")
        PM = P.ps([128, 512], F32, f"PM{d}")
        qT = [P.sb([128, D], BF16, f"qT{d}{k}") for k in range(2)]
        kT = [P.sb([128, D], BF16, f"kT{d}{k}") for k in range(2)]
        kk = [P.sb([128, D], BF16, f"kk{d}{k}") for k in range(2)]
        vv = [P.sb([128, D], BF16, f"vv{d}{k}") for k in range(2)]
        dec = [P.sb([128, 64], F32, f"dec{d}{k}") for k in range(2)]
        W = [P.sb([128, 128], F32, f"W{d}{h}") for h in range(8)]
        Sb = [P.sb([128, 128], BF16, f"Sb{d}{h}") for h in range(8)]
        g = [[P.sb([128, 8], F32, f"g{d}{k}{c}") for c in range(2)] for k in range(2)]
        endp = [P.sb([128, 8], F32, f"endp{d}{k}") for k in range(2)]
        ATs = [P.sb([128, 512], BF16, f"ATs{d}{k}") for k in range(2)]
        pat_sb = [P.sb([128, 512], F32, f"pat{d}{k}") for k in range(2)]
        osb = [P.sb([128, D], F32, f"osb{d}{k}") for k in range(2)]
        for h in range(8):
            S.op("dve", "memset", writes=[W[h]], ap=W[h][:], constant=0.0)
        order = list(range(34)) if d == 0 else [1, 0] + list(range(33, 1, -1))
        chunks = (0, 1) if d == 0 else (1, 0)
        prev_end = None
        na = 0

        def loads(ti):
            tt = order[ti]
            k = ti % 2
            S.op("sp", "dma_start", writes=[qT[k]], stream="ld", out=qT[k][:], in_=self.hq[d][tt])
            S.op("sp", "dma_start", writes=[kT[k]], stream="ld", out=kT[k][:], in_=self.hkT[d][tt])
            S.op("sp", "dma_start", writes=[kk[k]], stream="ld", out=kk[k][:], in_=self.hk[d][tt])
            S.op("sp", "dma_start", writes=[vv[k]], stream="ld", out=vv[k][:], in_=self.hv[tt])
            S.op("sp", "dma_start", writes=[dec[k]], stream="ld", out=dec[k][:], in_=self.hdec[tt])

        loads(0)
        for ti, tt in enumerate(order):
            k = ti % 2
            if ti + 1 < len(order):
                loads(ti + 1)
            dv = dec[k][:, d * 32:(d + 1) * 32].rearrange("p (h c e) -> p h c e", h=8, c=2)
            for ci, c in enumerate(chunks):
                gq = g[k][ci]
                if prev_end is None:
                    S.op("dve", "tensor_copy", reads=[dec[k]], writes=[gq], out=gq[:], in_=dv[:, :, c, 0])
                    prev_end = endp[0]
                elif ci == 0:
                    S.op("dve", "tensor_tensor", reads=[dec[k], prev_end], writes=[gq], out=gq[:],
                         in0=dv[:, :, c, 0], in1=prev_end[:], op=ALU.mult)
                else:
                    S.op("dve", "tensor_tensor", reads=[dec[k]], writes=[gq], out=gq[:], in0=dv[:, :, c, 0],
                         in1=dv[:, :, chunks[0], 1], op=ALU.mult)
            prev_end = endp[ti % 2]
            S.op("dve", "tensor_copy", reads=[dec[k]], writes=[prev_end], out=prev_end[:], in_=dv[:, :, chunks[1], 1])
            for hg in range(2):
                at = ATs[na % 2]
                na += 1
                for hh in range(4):
                    h = hg * 4 + hh
                    S.op("pe", "matmul", reads=[kT[k], qT[k]], writes=[PAT], out=PAT[:, hh * 128:(hh + 1) * 128],
                         lhsT=kT[k][:, h * 128:(h + 1) * 128], rhs=qT[k][:, h * 128:(h + 1) * 128], start=True,
                         stop=True)
                yield
                psb = pat_sb[na % 2]
                S.op("act", "copy", reads=[PAT], writes=[psb], out=psb[:], in_=PAT[:])
                S.op("pool", "affine_select", reads=[psb], writes=[at],
                     out=at[:, :].rearrange("p (h t) -> p h t", h=4), in_=psb[:, :].rearrange("p (h t) -> p h t", h=4),
                     pattern=[[0, 4], [1 if d == 0 else -1, 128]], compare_op=ALU.is_ge, fill=0.0, base=0,
                     channel_multiplier=(-1 if d == 0 else 1))
                if d == 0:
                    zb = at[0:64, :].rearrange("p (h t) -> p h t", h=4)[:, :, 64:128]
                else:
                    zb = at[64:128, :].rearrange("p (h t) -> p h t", h=4)[:, :, 0:64]
                S.op("pool", "memset", writes=[at], ap=zb, constant=0.0)
                yield
                for hh in range(4):
                    h = hg * 4 + hh
                    S.op("pe", "matmul", reads=[at, vv[k]], writes=[PO], out=PO[:, hh * 128:(hh + 1) * 128],
                         lhsT=at[:, hh * 128:(hh + 1) * 128], rhs=vv[k][:, h * 128:(h + 1) * 128],
                         start=True, stop=True)
                for ci, c in enumerate(chunks):
                    gq = g[k][ci]
                    for hh in range(4):
                        h = hg * 4 + hh
                        S.op("pe", "matmul", reads=[kk[k], vv[k]], writes=[PM], out=PM[:, hh * 128:(hh + 1) * 128],
                             lhsT=kk[k][c * 64:(c + 1) * 64, h * 128:(h + 1) * 128],
                             rhs=vv[k][c * 64:(c + 1) * 64, h * 128:(h + 1) * 128], start=True, stop=True)
                    for hh in range(4):
                        h = hg * 4 + hh
                        S.op("act", "activation", reads=[W[h], gq], writes=[Sb[h]], out=Sb[h][:], in_=W[h][:],
                             func=AF.Copy, scale=gq[:, h:h + 1])
                        S.op("pe", "matmul", reads=[qT[k], Sb[h]], writes=[PI],
                             out=PI[c * 64:(c + 1) * 64, hh * 128:(hh + 1) * 128],
                             lhsT=qT[k][:, h * 128 + c * 64: h * 128 + (c + 1) * 64], rhs=Sb[h][:], start=True,
                             stop=True)
                        S.op("dve", "scalar_tensor_tensor", reads=[W[h], gq, PM], writes=[W[h]], out=W[h][:],
                             in0=W[h][:], scalar=gq[:, h:h + 1], op0=ALU.mult, in1=PM[:, hh * 128:(hh + 1) * 128],
                             op1=ALU.add)
                    yield
                S.op("act", "copy", reads=[PO], writes=[osb[k]], out=osb[k][:, hg * 512:(hg + 1) * 512], in_=PO[:])
                S.op("dve", "tensor_tensor", reads=[PI, osb[k]], writes=[osb[k]], out=osb[k][:, hg * 512:(hg + 1) * 512],
                     in0=PI[:], in1=osb[k][:, hg * 512:(hg + 1) * 512], op=ALU.add)
                yield
            S.op("sp", "dma_start", reads=[osb[k]], stream="st", out=self.ho[d][tt], in_=osb[k][:])

    def phase_h4(self, S, i, do_ctx):
        nc = self.nc
        j = 1
        jm = i // 2
        with Phase(nc) as P:
            who = P.sb([128, 8 * D], BF16, "who")
            S.op("sp", "dma_start", writes=[who], stream="wt", out=who[:], in_=self.whos[jm])
            junk = P.sb([128, D], F32, "junk")
            PY = P.ps([128, D], F32, "PY")
            PK = P.ps([128, D], BF16, "PK")
            PG = P.ps([128, 128], F32, "PG")
            Cbc = [P.sb([128, D], F32, f"Cbc{v}") for v in range(2)]
            for v in range(2 if do_ctx else 1):
                self.make_bc(S, P, self.colv(self.Ccol, i, j, v), [self.Ccol], Cbc[v], junk, PY)
            gnc = P.sb([128, 2], F32, "gnc")
            gnbc = P.sb([128, 128], F32, "gnbc")
            S.op("sp", "dma_start", writes=[gnc], stream="ld", out=gnc[:], in_=self.gncol)
            S.op("dve", "tensor_scalar", reads=[self.ident, gnc], writes=[junk], out=junk[:, 0:128], in0=self.ident[:],
                 scalar1=gnc[:, jm:jm + 1], scalar2=None, op0=ALU.mult)
            S.op("pe", "matmul", reads=[self.ones, junk], writes=[PG], out=PG[:], lhsT=self.ones[:], rhs=junk[:, 0:128],
                 start=True, stop=True)
            S.op("act", "copy", reads=[PG], writes=[gnbc], out=gnbc[:], in_=PG[:])
            of = [P.sb([128, D], F32, f"of{k}") for k in range(3)]
            ob = [P.sb([128, D], F32, f"ob{k}") for k in range(3)]
            sg = [P.sb([128, D], F32, f"sg{k}") for k in range(3)]
            xb = [P.sb([128, D], F32, f"xb{k}") for k in range(3)]
            sq = [P.sb([128, D], F32, f"sq{k}") for k in range(2)]
            on = [P.sb([128, D], F32, f"on{k}") for k in range(2)]
            og = [P.sb([128, D], BF16, f"og{k}") for k in range(2)]
            ogT = [P.sb([128, D], BF16, f"ogT{k}") for k in range(2)]
            tm = [P.sb([128, D], F32, f"tm{k}") for k in range(2)]
            s8 = [P.sb([128, 24], F32, f"s8{k}") for k in range(2)]
            st = [P.sb([128, 4], F32, f"st{k}") for k in range(4)]
            tiles = list(range(0 if do_ctx else 2, 34))
            onb = [[Buf(f"on{k}{h}") for h in range(8)] for k in range(2)]

            def h4loads(ti):
                tt = tiles[ti]
                m = ti % 3
                S.op("sp", "dma_start", writes=[of[m]], stream="ld", out=of[m][:], in_=self.ho[0][tt])
                S.op("sp", "dma_start", writes=[ob[m]], stream="ld", out=ob[m][:], in_=self.ho[1][tt])
                S.op("sp", "dma_start", writes=[sg[m]], stream="ld", out=sg[m][:], in_=self.hsg[tt])
                S.op("sp", "dma_start", writes=[xb[m]], stream="ld", out=xb[m][:], in_=self.tile_rows(tt))

            def front(ti):
                tt = tiles[ti]
                k = ti % 2
                m = ti % 3
                S.op("dve", "tensor_tensor", reads=[of[m], ob[m]], writes=[of[m]], out=of[m][:], in0=of[m][:],
                     in1=ob[m][:], op=ALU.add)
                S.op("act", "activation", reads=[of[m]], writes=[sq[k]], out=sq[k][:], in_=of[m][:], func=AF.Square)
                s = s8[k]
                S.op("dve", "tensor_reduce", reads=[sq[k]], writes=[s], out=s[:, 0:8],
                     in_=sq[k][:, :].rearrange("p (h v) -> p h v", h=8), op=ALU.add, axis=mybir.AxisListType.X)
                S.op("act", "activation", reads=[s, self.epsc], writes=[s], out=s[:, 8:16], in_=s[:, 0:8], func=AF.Sqrt,
                     scale=1.0 / 128, bias=self.epsc[:, 0:1])
                S.op("dve", "reciprocal", reads=[s], writes=[s], out=s[:, 16:24], in_=s[:, 8:16])
                for h in range(8):
                    S.op("dve", "scalar_tensor_tensor", reads=[of[m], s, gnbc], writes=[onb[k][h]],
                         out=on[k][:, h * 128:(h + 1) * 128], in0=of[m][:, h * 128:(h + 1) * 128],
                         scalar=s[:, 16 + h:17 + h], op0=ALU.mult, in1=gnbc[:], op1=ALU.mult)
                S.op("dve", "tensor_tensor", reads=onb[k] + [sg[m]], writes=[og[k]], out=og[k][:], in0=on[k][:],
                     in1=sg[m][:], op=ALU.mult)

            def back(ti):
                tt = tiles[ti]
                k = ti % 2
                v = 1 if tt < 2 else 0
                rows = self.tile_rows(tt)
                for kc in range(8):
                    S.op("pe", "transpose", reads=[og[k], self.identb], writes=[PK], out=PK[:, kc * 128:(kc + 1) * 128],
                         in_=og[k][:, kc * 128:(kc + 1) * 128], identity=self.identb[:])
                S.op("act", "copy", reads=[PK], writes=[ogT[k]], out=ogT[k][:], in_=PK[:])
                for dh in range(2):
                    for kc in range(8):
                        S.op("pe", "matmul", reads=[ogT[k], who], writes=[PY], out=PY[:, dh * 512:(dh + 1) * 512],
                             lhsT=ogT[k][:, kc * 128:(kc + 1) * 128],
                             rhs=who[:, kc * D + dh * 512: kc * D + (dh + 1) * 512], start=(kc == 0), stop=(kc == 7))
                stt = st[ti % 4]
                S.op("act", "activation", reads=[PY], writes=[junk, stt], out=junk[:], in_=PY[:], func=AF.Square,
                     accum_out=stt[:, 0:1])
                self.rstd_from_ss(S, stt)
                S.op("dve", "scalar_tensor_tensor", reads=[PY, stt, Cbc[v]], writes=[tm[k]], out=tm[k][:], in0=PY[:],
                     scalar=stt[:, 2:3], op0=ALU.mult, in1=Cbc[v][:], op1=ALU.mult)
                S.op("dve", "tensor_tensor", reads=[tm[k], xb[ti % 3]], writes=[tm[k]], out=tm[k][:], in0=tm[k][:],
                     in1=xb[ti % 3][:], op=ALU.add)
                S.op("sp", "dma_start", reads=[tm[k]], stream="st", out=rows, in_=tm[k][:])

            h4loads(0)
            h4loads(1)
            front(0)
            for ti in range(len(tiles)):
                if ti + 2 < len(tiles):
                    h4loads(ti + 2)
                if ti + 1 < len(tiles):
                    front(ti + 1)
                back(ti)
            S.emit_phase(f"h4_{i}")


def build(stop_after=None, layers=(0, 1, 2, 3)):
    nc = bass.Bass("TRN2", target_bir_lowering=False)
    pr = Prog(nc, stop_after)
    with contextlib.ExitStack() as es:
        S = Sched(nc, es)
        pr.alloc_persistent(es)
        with nc.Block() as block:
            @block.sync
            def _(e):
                for k in S.keys:
                    e.sem_clear(S.sem[k])
        pr.phase_init(S)
        pr.make_jobs(layers)
        pr.phase_prologue(S, layers)
        for i in layers:
            first = i == layers[0]
            last = i == DEPTH - 1
            pr.ensure_unit(S, ("F", i, 0))
            pr.phase_ffn(S, i, 0, True, pr.x if first else pr.y, pr.ctx if first else pr.cs)
            if stop_after == ("ffn", i, 0):
                break
            pr.ensure_unit(S, ("M", i))
            if i % 2 == 0:
                pr.phase_fourier1(S, i)
                pr.phase_fourier2(S, i, not last)
            else:
                pr.phase_h1(S, i)
                if stop_after == ("h1", i):
                    break
                pr.phase_hscan(S, i)
                if stop_after == ("hs1", i):
                    break
                pr.phase_h4(S, i, not last)
            if stop_after == ("mix", i):
                break
            pr.ensure_unit(S, ("F", i, 1))
            pr.phase_ffn(S, i, 1, not last, pr.y)
    return nc


def host_inputs(b, inp):
    f = np.float32
    m = {}
    m["x"] = np.ascontiguousarray(inp["x"][b])
    m["ctx"] = np.ascontiguousarray(inp["ctx"][b])
    cc = np.stack([inp["c"][b].reshape(8, 128).T, inp["c_ctx"].reshape(8, 128).T], axis=-1)
    m["ccol"] = np.ascontiguousarray(cc.reshape(128, 16)).astype(f)
    m["ada_w"] = inp["ada_w"]
    m["ada_b_col"] = np.ascontiguousarray(inp["ada_b"].reshape(DEPTH, 72, 128).transpose(2, 0, 1).reshape(128, -1))
    m["npre_col"] = np.ascontiguousarray(inp["norm_pre"].reshape(DEPTH, 3, 8, 128).transpose(3, 0, 1, 2).reshape(128, -1))
    m["npost_col"] = np.ascontiguousarray(inp["norm_post"].reshape(DEPTH, 3, 8, 128).transpose(3, 0, 1, 2).reshape(128, -1))
    return m


_SHARED = {}


def host_shared(inp):
    w = inp["ffn_w_in"]
    idx = (np.arange(2)[None, :, None] * DFF + np.arange(NFC)[:, None, None] * 128 + np.arange(128)[None, None, :]).reshape(-1)
    m = {"ffn_w_in_p": np.ascontiguousarray(w[..., idx]), "ffn_w_out": inp["ffn_w_out"],
         "fourier_w_out": inp["fourier_w_out"], "hgrn_w_in": inp["hgrn_w_in"], "hgrn_w_out": inp["hgrn_w_out"]}
    lb = np.stack([inp["hgrn_lb_fwd"], inp["hgrn_lb_bwd"]], 0).reshape(2, 2, 8, 128)
    m["lbcol"] = np.ascontiguousarray(lb.transpose(3, 0, 1, 2).reshape(128, 32))
    m["gncol"] = np.ascontiguousarray(inp["hgrn_norm"].T)
    m.update(host_consts())
    return m


def host_consts():
    r = np.arange(64)
    a = 2 * np.pi * np.outer(r, r) / 64
    c64, s64 = np.cos(a) / 8, np.sin(a) / 8
    z = np.zeros((64, 64))
    bd = lambda m: np.block([[m, z], [z, m]])
    k = np.arange(256)
    a = 2 * np.pi * np.outer(k, k) / 256
    c256, s256 = np.cos(a) / 16, np.sin(a) / 16
    f2 = lambda m: m.reshape(2, 128, 256).transpose(1, 0, 2).reshape(128, 512)
    fc = np.concatenate([bd(c64), bd(s64), -bd(s64), f2(c256), f2(s256), -f2(s256)], axis=1)
    s_ = np.arange(128)[:, None]
    t_ = np.arange(128)[None, :]
    same = (s_ // CH) == (t_ // CH)
    mf = (t_ // CH) * CH + MID
    mb = (t_ // CH) * CH + MID + 1
    Dmf = same * ((s_ <= t_).astype(np.float64) - (s_ <= mf))
    Dmb = same * ((s_ >= t_).astype(np.float64) - (s_ >= mb))
    Ddf = np.zeros((128, 4))
    Ddb = np.zeros((128, 4))
    sv = np.arange(128)
    for c in range(2):
        inc = (sv // CH) == c
        Ddf[:, 2 * c] = inc & (sv <= c * CH + MID)
        Ddf[:, 2 * c + 1] = inc & (sv > c * CH + MID)
        Ddb[:, 2 * c] = inc & (sv >= c * CH + MID + 1)
        Ddb[:, 2 * c + 1] = inc & (sv < c * CH + MID + 1)
    BIG = 1e30
    vf = same & (s_ <= t_)
    vb = same & (s_ >= t_)
    hcst = np.concatenate([Dmf, Ddf, Dmb, Ddb, vf * BIG, vf * -BIG, vb * BIG, vb * -BIG], axis=1)
    return {"fconst": np.ascontiguousarray(fc).astype(np.float32),
            "hconst": np.ascontiguousarray(hcst).astype(np.float32)}


def kernel(**inp):
    inp = {k: np.asarray(v) for k, v in inp.items()}
    nc = build()
    sh = host_shared(inp)
    in_maps = []
    for b in range(8):
        m = host_inputs(b, inp)
        m.update(sh)
        in_maps.append(m)
    res = run_bass_kernel_spmd(nc, in_maps, core_ids=list(range(8)))
    return np.stack([r["y"] for r in res.results], axis=0)
```

```python
import bisect
import contextlib
import numpy as np
import concourse.bass as bass
import concourse.mybir as mybir
from concourse.bass_utils import run_bass_kernel_spmd

F32 = mybir.dt.float32
BF16 = mybir.dt.bfloat16
AF = mybir.ActivationFunctionType
ALU = mybir.AluOpType

D = 1024
L = 4096
LC = 256
DEPTH = 4
DFF = 2816
NFC = DFF // 128
EPS = 1e-6
CH = 64
MID = CH // 2 - 1


class Buf:
    __slots__ = ("name", "lw", "rd")

    def __init__(self, name):
        self.name = name
        self.lw = None
        self.rd = {}


class T:
    def __init__(self, t, name):
        self.t = t
        self.b = Buf(name)

    def __getitem__(self, idx):
        return self.t[idx]


ENGS = ("pe", "act", "dve", "pool", "sp")
SAME_ENG_SYNC = True
NDS = 48
STREAMS = tuple(f"d{k}" for k in range(NDS))


class Sched:
    def __init__(self, nc, es):
        self.nc = nc
        self.keys = list(ENGS) + list(STREAMS)
        self.sem = {k: es.enter_context(nc.semaphore("s_" + k)) for k in self.keys}
        self.cnt = {k: 0 for k in self.keys}
        self.base = {k: 0 for k in self.keys}
        self.sigbase = {k: 0 for k in self.keys}
        self.q = {e: [] for e in ENGS}
        self.sig = {k: set() for k in self.keys}
        self.last_dma_eng = {}
        self.bufkey = {}

    def op(self, eng, meth, reads=(), writes=(), stream=None, **kw):
        if stream:
            kb = writes[0] if writes else reads[0]
            kb = kb.b if isinstance(kb, T) else kb
            if id(kb) not in self.bufkey:
                self.bufkey[id(kb)] = STREAMS[len(self.bufkey)]
            stream = self.bufkey[id(kb)]
        key = stream if stream else eng
        deps = set()
        for x in reads:
            b = x.b if isinstance(x, T) else x
            if b.lw:
                deps.add(b.lw)
        for x in writes:
            b = x.b if isinstance(x, T) else x
            if b.lw:
                deps.add(b.lw)
            deps.update(b.rd.items())
        if stream is None and (eng == "pe" or not SAME_ENG_SYNC):
            deps = {d for d in deps if d[0] != eng}
        deps = {d for d in deps if d[1] > self.base[d[0]]}
        self.cnt[key] += 1
        n = self.cnt[key]
        tok = (key, n)
        wb = []
        for x in writes:
            b = x.b if isinstance(x, T) else x
            b.lw = tok
            b.rd = {}
            wb.append(b)
        for x in reads:
            b = x.b if isinstance(x, T) else x
            if b not in wb:
                b.rd[key] = n
        self.q[eng].append((meth, kw, deps, tok, stream is not None))
        for d in deps:
            self.sig[d[0]].add(d[1])
        if stream:
            self.last_dma_eng[stream] = eng
        return tok

    def emit_phase(self, name):
        nc = self.nc
        sigl = {k: sorted(v) for k, v in self.sig.items()}

        def sigval(k, n):
            if k in STREAMS:
                return 16 * n
            return self.sigbase[k] + bisect.bisect_right(sigl[k], n)

        with nc.Block() as block:
            deco = {"pe": block.tensor, "act": block.scalar, "dve": block.vector, "pool": block.gpsimd,
                    "sp": block.sync}
            for eng in ENGS:
                q = self.q[eng]
                tail = [s for s in STREAMS if self.last_dma_eng.get(s) == eng and self.cnt[s] > self.base[s]]
                if not q and not tail:
                    continue

                def body(e, q=q, eng=eng, tail=tail):
                    seen = {}
                    for meth, kw, deps, tok, isdma in q:
                        for (k, n) in sorted(deps):
                            v = sigval(k, n)
                            if seen.get(k, 0) < v:
                                e.wait_ge(self.sem[k], v)
                                seen[k] = v
                        ins = getattr(e, meth)(**kw)
                        key, n = tok
                        if isdma:
                            ins.then_inc(self.sem[key], 16)
                        elif n in self.sig[key]:
                            ins.then_inc(self.sem[key], 1)
                    for s in tail:
                        e.wait_ge(self.sem[s], 16 * self.cnt[s])

                deco[eng](body)
        for k in self.keys:
            self.sigbase[k] += len(self.sig[k]) if k not in STREAMS else 0
            self.sig[k] = set()
            self.base[k] = self.cnt[k]
        self.q = {e: [] for e in ENGS}
        self.last_dma_eng = {}
        self.bufkey = {}


class Phase:
    G = 0

    def __init__(self, nc):
        self.nc = nc
        self.es = contextlib.ExitStack()
        self.n = 0

    def __enter__(self):
        self.es.__enter__()
        return self

    def __exit__(self, *a):
        return self.es.__exit__(*a)

    def sb(self, shape, dt, name=None):
        Phase.G += 1
        name = (name or "t") + f"_{Phase.G}"
        return T(self.es.enter_context(self.nc.sbuf_tensor(name, list(shape), dt)), name)

    def ps(self, shape, dt=F32, name=None):
        Phase.G += 1
        name = (name or "p") + f"_{Phase.G}"
        return T(self.es.enter_context(self.nc.psum_tensor(name, list(shape), dt)), name)


class Prog:
    def __init__(self, nc, stop_after=None):
        self.nc = nc
        self.stop_after = stop_after
        dt = nc.dram_tensor
        self.x = dt("x", [L, D], F32, kind="ExternalInput").ap()
        self.ctx = dt("ctx", [LC, D], F32, kind="ExternalInput").ap()
        self.ccol = dt("ccol", [128, 16], F32, kind="ExternalInput").ap()
        self.ada_w = dt("ada_w", [DEPTH, D, 9 * D], F32, kind="ExternalInput").ap()
        self.ada_b_col = dt("ada_b_col", [128, DEPTH * 72], F32, kind="ExternalInput").ap()
        self.npre_col = dt("npre_col", [128, DEPTH * 3 * 8], F32, kind="ExternalInput").ap()
        self.npost_col = dt("npost_col", [128, DEPTH * 3 * 8], F32, kind="ExternalInput").ap()
        self.w_in = dt("ffn_w_in_p", [DEPTH, 2, D, 2 * DFF], F32, kind="ExternalInput").ap()
        self.w_out = dt("ffn_w_out", [DEPTH, 2, DFF, D], F32, kind="ExternalInput").ap()
        self.y = dt("y", [L, D], F32, kind="ExternalOutput").ap()
        self.cs = dt("cs", [LC, D], F32, kind="ExternalOutput" if stop_after else "Internal").ap()
        self.fconst = dt("fconst", [128, 1920], F32, kind="ExternalInput").ap()
        self.fw = dt("fourier_w_out", [2, D, D], F32, kind="ExternalInput").ap()
        self.wfs = dt("wfs", [2, 128, 8 * D], BF16, kind="Internal").ap()
        self.uv = dt("uv", [L, 2 * D], BF16, kind="Internal").ap()
        self.hconst = dt("hconst", [128, 776], F32, kind="ExternalInput").ap()
        self.lbcol = dt("lbcol", [128, 32], F32, kind="ExternalInput").ap()
        self.gncol = dt("gncol", [128, 2], F32, kind="ExternalInput").ap()
        self.hwin = dt("hgrn_w_in", [2, D, 5 * D], F32, kind="ExternalInput").ap()
        self.hwout = dt("hgrn_w_out", [2, D, D], F32, kind="ExternalInput").ap()
        self.wh1s = dt("wh1s", [2, 128, 8 * 5 * D], BF16, kind="Internal").ap()
        self.whos = dt("whos", [2, 128, 8 * D], BF16, kind="Internal").ap()
        NT = 34
        dbgk = "ExternalOutput" if stop_after else "Internal"
        self.hq = [dt(f"hq{d}", [NT, 128, D], BF16, kind=dbgk).ap() for d in range(2)]
        self.hkT = [dt(f"hkT{d}", [NT, 128, D], BF16, kind=dbgk).ap() for d in range(2)]
        self.hk = [dt(f"hk{d}", [NT, 128, D], BF16, kind=dbgk).ap() for d in range(2)]
        self.hv = dt("hv", [NT, 128, D], BF16, kind=dbgk).ap()
        self.hsg = dt("hsg", [NT, 128, D], F32, kind=dbgk).ap()
        self.hdec = dt("hdec", [NT, 128, 64], F32, kind=dbgk).ap()
        self.ho = [dt(f"ho{d}", [NT, 128, D], F32, kind="ExternalOutput" if (stop_after and d == 0) else "Internal").ap() for d in range(2)]
        self.w1s = dt("w1s", [DEPTH, 2, 11, 128, 8 * 512], BF16, kind="Internal").ap()
        self.w2s = dt("w2s", [DEPTH, 2, 128, NFC * D], BF16, kind="Internal").ap()

    def alloc_persistent(self, es):
        nc = self.nc

        def sb(name, shape, dt):
            return T(es.enter_context(nc.sbuf_tensor(name, list(shape), dt)), name)

        self.ident = sb("ident", [128, 128], F32)
        self.identb = sb("identb", [128, 128], BF16)
        self.ones = sb("ones", [128, 128], F32)
        self.epsc = sb("epsc", [128, 1], F32)
        self.mod = sb("mod", [128, DEPTH * 2 * 72], F32)
        self.Acol = sb("Acol", [128, DEPTH * 3 * 2 * 8], F32)
        self.Ccol = sb("Ccol", [128, DEPTH * 3 * 2 * 8], F32)
        self.gpre = sb("gpre", [128, DEPTH * 3 * 8], F32)
        self.gpost = sb("gpost", [128, DEPTH * 3 * 8], F32)

    def modv(self, i, v, m):
        o = ((i * 2 + v) * 9 + m) * 8
        return self.mod[:, o:o + 8]

    def colv(self, tt, i, j, v):
        o = ((i * 3 + j) * 2 + v) * 8
        return tt[:, o:o + 8]

    def phase_init(self, S):
        nc = self.nc
        with Phase(nc) as P:
            iot = P.sb([128, 128], F32)
            pid = P.sb([128, 1], F32)
            S.op("pool", "iota", writes=[iot], out=iot[:], pattern=[[1, 128]], base=0, channel_multiplier=0,
                 allow_small_or_imprecise_dtypes=True)
            S.op("pool", "iota", writes=[pid], out=pid[:], pattern=[[1, 1]], base=0, channel_multiplier=1,
                 allow_small_or_imprecise_dtypes=True)
            S.op("dve", "tensor_scalar", reads=[iot, pid], writes=[self.ident], out=self.ident[:], in0=iot[:],
                 scalar1=pid[:, 0:1], scalar2=None, op0=ALU.is_equal)
            S.op("dve", "tensor_copy", reads=[self.ident], writes=[self.identb], out=self.identb[:],
                 in_=self.ident[:])
            S.op("dve", "memset", writes=[self.ones], ap=self.ones[:], constant=1.0)
            S.op("dve", "memset", writes=[self.epsc], ap=self.epsc[:], constant=EPS)
            S.op("sp", "dma_start", writes=[self.gpre], stream="ld", out=self.gpre[:], in_=self.npre_col)
            S.op("sp", "dma_start", writes=[self.gpost], stream="ld", out=self.gpost[:], in_=self.npost_col)
            S.emit_phase("init")

    def make_jobs(self, layers):
        jobs = []

        def add(unit, src, dst, a, c):
            if a * c == 4096:
                h = a // 2
                jobs.append((unit, src[:, 0:h, :], dst[:, 0:h, :], h, c))
                jobs.append((unit, src[:, h:a, :], dst[:, h:a, :], h, c))
            else:
                jobs.append((unit, src, dst, a, c))

        def ffn(i, f):
            u = ("F", i, f)
            for g in range(11):
                src = self.w_in[i, f, :, g * 512:(g + 1) * 512].rearrange("(kc p) c -> p kc c", p=128)
                add(u, src, self.w1s[i, f, g].rearrange("p (a c) -> p a c", a=8), 8, 512)
            for g in range(11):
                src = self.w_out[i, f, g * 256:(g + 1) * 256, :].rearrange("(fc p) c -> p fc c", p=128)
                add(u, src, self.w2s[i, f, :, g * 2048:(g + 1) * 2048].rearrange("p (a c) -> p a c", a=2), 2, 1024)

        for i in layers:
            ffn(i, 0)
            u = ("M", i)
            if i % 2 == 1:
                for cg in range(10):
                    src = self.hwin[i // 2, :, cg * 512:(cg + 1) * 512].rearrange("(kc p) c -> p kc c", p=128)
                    dst = self.wh1s[i // 2].rearrange("p (kc n) -> p kc n", kc=8)[:, :, cg * 512:(cg + 1) * 512]
                    add(u, src, dst, 8, 512)
                for h in range(2):
                    src = self.hwout[i // 2, :, h * 512:(h + 1) * 512].rearrange("(kc p) c -> p kc c", p=128)
                    dst = self.whos[i // 2].rearrange("p (kc n) -> p kc n", kc=8)[:, :, h * 512:(h + 1) * 512]
                    add(u, src, dst, 8, 512)
            else:
                for h in range(2):
                    src = self.fw[i // 2, :, h * 512:(h + 1) * 512].rearrange("(kc p) c -> p kc c", p=128)
                    dst = self.wfs[i // 2].rearrange("p (kc n) -> p kc n", kc=8)[:, :, h * 512:(h + 1) * 512]
                    add(u, src, dst, 8, 512)
            ffn(i, 1)
        self.jobs = jobs
        self.jnext = 0

    def jobs_pending_for(self, unit):
        last = max([k for k, j in enumerate(self.jobs) if j[0] == unit], default=-1)
        return max(0, last + 1 - self.jnext)

    class BG:
        def __init__(self, pr, S, P):
            self.pr, self.S = pr, S
            self.s32 = [P.sb([128, 2048], F32, f"bg32_{k}") for k in range(2)]
            self.s16 = [P.sb([128, 2048], BF16, f"bg16_{k}") for k in range(2)]
            self.loaded = None
            self.casted = None

        def _load(self, jk):
            unit, src, dst, a, c = self.pr.jobs[jk]
            t = self.s32[jk % 2]
            self.S.op("sp", "dma_start", writes=[t], stream="ld", out=t[:, :].rearrange("p (a c) -> p a c", a=a),
                      in_=src)

        def _finish(self, jk, eng="pool"):
            unit, src, dst, a, c = self.pr.jobs[jk]
            t32, t16 = self.s32[jk % 2], self.s16[jk % 2]
            self.S.op(eng, "tensor_copy", reads=[t32], writes=[t16], out=t16[:], in_=t32[:])
            self.S.op("sp", "dma_start", reads=[t16], stream="st", out=dst,
                      in_=t16[:, :].rearrange("p (a c) -> p a c", a=a))

        def _cast(self, jk, eng):
            t32, t16 = self.s32[jk % 2], self.s16[jk % 2]
            self.S.op(eng, "copy" if eng == "act" else "tensor_copy", reads=[t32], writes=[t16], out=t16[:], in_=t32[:])

        def _store(self, jk):
            unit, src, dst, a, c = self.pr.jobs[jk]
            t16 = self.s16[jk % 2]
            self.S.op("sp", "dma_start", reads=[t16], stream="st", out=dst,
                      in_=t16[:, :].rearrange("p (a c) -> p a c", a=a))

        def step(self, eng="pool"):
            pr = self.pr
            nxt = pr.jnext if pr.jnext < len(pr.jobs) else None
            if nxt is not None:
                self._load(nxt)
                pr.jnext += 1
            if self.casted is not None:
                self._store(self.casted)
            if self.loaded is not None:
                self._cast(self.loaded, eng)
            self.casted = self.loaded
            self.loaded = nxt

        def flush(self, eng="pool"):
            while self.loaded is not None or self.casted is not None:
                if self.casted is not None:
                    self._store(self.casted)
                if self.loaded is not None:
                    self._cast(self.loaded, eng)
                self.casted = self.loaded
                self.loaded = None

    def ensure_unit(self, S, unit):
        n = self.jobs_pending_for(unit)
        if n == 0:
            return
        with Phase(self.nc) as P:
            bg = Prog.BG(self, S, P)
            engs = ("pool", "dve")
            for k in range(n + 1):
                bg.step(engs[k % 2]) if k < n else bg.flush(engs[k % 2])
            S.emit_phase("cvt")

    def phase_prologue(self, S, layers):
        nc = self.nc
        with Phase(nc) as P:
            bg = Prog.BG(self, S, P)
            ncv = self.jobs_pending_for(("F", layers[0], 0))
            cc = P.sb([128, 16], F32)
            s2 = P.sb([128, 16], F32)
            bcol = P.sb([128, DEPTH * 72], F32)
            awb = [P.sb([128, 8 * 512], F32, f"awb{k}") for k in range(3)]
            modrow = P.sb([2, 9 * D], F32, "modrow")
            pr_ = [P.ps([128, 512], F32, f"pr{k}") for k in range(2)]
            pT = [P.ps([128, 512], F32, f"pT{k}") for k in range(2)]
            S.op("sp", "dma_start", writes=[cc], stream="ld", out=cc[:], in_=self.ccol)
            S.op("sp", "dma_start", writes=[bcol], stream="ld", out=bcol[:], in_=self.ada_b_col)
            S.op("act", "activation", reads=[cc], writes=[s2], out=s2[:], in_=cc[:], func=AF.Silu)
            ajobs = [(i, nn) for i in layers for nn in range(18)]

            def aload(k):
                i, nn = ajobs[k]
                S.op("sp", "dma_start", writes=[awb[k % 3]], stream="wt",
                     out=awb[k % 3][:, :].rearrange("p (kc n) -> p kc n", kc=8),
                     in_=self.ada_w[i, :, nn * 512:(nn + 1) * 512].rearrange("(kc p) n -> p kc n", p=128))

            aload(0)
            aload(1)
            cv = 0
            engs = ("pool", "dve")
            for k, (i, nn) in enumerate(ajobs):
                if k + 2 < len(ajobs):
                    aload(k + 2)
                while cv < ncv and cv * len(ajobs) <= k * ncv:
                    bg.step(engs[cv % 2])
                    cv += 1
                w = awb[k % 3]
                ps = pr_[k % 2]
                for kc in range(8):
                    S.op("pe", "matmul", reads=[w, s2], writes=[ps], out=ps[0:2, :], lhsT=s2[:, kc * 2:kc * 2 + 2],
                         rhs=w[:, kc * 512:(kc + 1) * 512], start=(kc == 0), stop=(kc == 7))
                S.op("act", "copy", reads=[ps], writes=[modrow], out=modrow[0:2, nn * 512:(nn + 1) * 512],
                     in_=ps[0:2, :])
                if nn == 17:
                    pt = pT[i % 2]
                    for ch in range(72):
                        S.op("pe", "transpose", reads=[modrow, self.ident], writes=[pt], out=pt[:, ch * 2:ch * 2 + 2],
                             in_=modrow[0:2, ch * 128:(ch + 1) * 128], identity=self.ident[0:2, 0:2])
                    for v in range(2):
                        o = (i * 2 + v) * 72
                        S.op("dve", "tensor_tensor", reads=[pt, bcol], writes=[self.mod],
                             out=self.mod[:, o:o + 72],
                             in0=pt[:, 0:144].rearrange("p (n v) -> p n v", v=2)[:, :, v],
                             in1=bcol[:, i * 72:(i + 1) * 72], op=ALU.add)
            while cv < ncv:
                bg.step(engs[cv % 2])
                cv += 1
            bg.flush()
            for i in layers:
                for j in range(3):
                    w = 1.0 if j == 1 else 0.5
                    g0 = (i * 3 + j) * 8
                    for v in range(2):
                        S.op("dve", "scalar_tensor_tensor", reads=[self.mod, self.gpre], writes=[self.Acol],
                             out=self.colv(self.Acol, i, j, v), in0=self.modv(i, v, 3 * j + 1), scalar=1.0,
                             op0=ALU.add, in1=self.gpre[:, g0:g0 + 8], op1=ALU.mult)
                        S.op("dve", "scalar_tensor_tensor", reads=[self.mod, self.gpost], writes=[self.Ccol],
                             out=self.colv(self.Ccol, i, j, v), in0=self.modv(i, v, 3 * j + 2), scalar=w,
                             op0=ALU.mult, in1=self.gpost[:, g0:g0 + 8], op1=ALU.mult)
            if self.stop_after is not None:
                dbg = self.nc.dram_tensor("dbg", [128, DEPTH * 2 * 72], F32, kind="ExternalOutput").ap()
                S.op("sp", "dma_start", reads=[self.mod], stream="st", out=dbg, in_=self.mod[:])
            S.emit_phase("prologue")

    def make_bc(self, S, P, col_ap, col_reads, out_t, scratch, ps):
        for kc in range(8):
            S.op("dve", "tensor_scalar", reads=[self.ident] + col_reads, writes=[scratch],
                 out=scratch[:, kc * 128:(kc + 1) * 128], in0=self.ident[:], scalar1=col_ap[:, kc:kc + 1],
                 scalar2=None, op0=ALU.mult)
        for h in range(2):
            S.op("pe", "matmul", reads=[self.ones, scratch], writes=[ps], out=ps[:, h * 512:(h + 1) * 512],
                 lhsT=self.ones[:], rhs=scratch[:, h * 512:(h + 1) * 512], start=True, stop=True)
        S.op("act", "copy", reads=[ps], writes=[out_t], out=out_t[:], in_=ps[:])

    def rstd_from_ss(self, S, st):
        S.op("act", "activation", reads=[st, self.epsc], writes=[st], out=st[:, 1:2], in_=st[:, 0:1], func=AF.Sqrt,
             scale=1.0 / D, bias=self.epsc[:, 0:1])
        S.op("dve", "reciprocal", reads=[st], writes=[st], out=st[:, 2:3], in_=st[:, 1:2])

    def phase_ffn(self, S, i, f, do_ctx, x_src, c_src=None):
        nc = self.nc
        j = 0 if f == 0 else 2
        with Phase(nc) as P:
            w2 = P.sb([128, NFC * D], BF16, "w2")
            NW1 = 4
            w1 = [P.sb([128, 8 * 512], BF16, f"w1_{k}") for k in range(NW1)]
            hTs = [P.sb([128, 8 * 512], BF16, f"hT{k}") for k in range(2)]
            actT = P.sb([128, NFC * 512], BF16, "actT")
            xa = [P.sb([128, D], F32, f"xa{k}") for k in range(2)]
            xn = [P.sb([128, D], F32, f"xn{k}") for k in range(2)]
            xb = [P.sb([128, D], F32, f"xb{k}") for k in range(2)]
            tmp = [P.sb([128, D], F32, f"tmp{k}") for k in range(2)]
            junk = P.sb([128, D], F32, "junk")
            sg = [P.sb([128, 512], F32, f"sg{k}") for k in range(2)]
            Cbc = [P.sb([128, D], F32, f"Cbc{v}") for v in range(2)]
            sta = [P.sb([128, 4], F32, f"sta{k}") for k in range(4)]
            stb = [P.sb([128, 4], F32, f"stb{k}") for k in range(4)]
            GU = [[P.ps([128, 512], F32, f"gu{a}{b}") for b in range(2)] for a in range(2)]
            YT = [P.ps([128, D], F32, f"yt{k}") for k in range(2)]

            bg = Prog.BG(self, S, P)
            nv = 2 if do_ctx else 1
            for v in range(nv):
                self.make_bc(S, P, self.colv(self.Ccol, i, j, v), [self.Ccol], Cbc[v], junk, YT[v])

            blocks = [(x_src, self.y, b * 512, 4, 0) for b in range(8)]
            if do_ctx:
                csrc = c_src if c_src is not None else self.cs
                blocks.append((csrc, self.cs, 0, 2, 1))
            nb = len(blocks)
            wjobs = [(bi, g) for bi in range(nb) for g in range(11)]
            wstate = {"next": 0}

            def w1_prefetch(upto):
                while wstate["next"] <= min(upto, len(wjobs) - 1):
                    k = wstate["next"]
                    S.op("sp", "dma_start", writes=[w1[k % NW1]], stream="wt", out=w1[k % NW1][:],
                         in_=self.w1s[i, f, wjobs[k][1]])
                    wstate["next"] += 1

            cnt = {"a": 0, "y": 0, "gu": 0, "st": 0}

            s1state = {}

            def stage1(bi, tiles=None, part="AB"):
                src, dst, tok0, nt, v = blocks[bi]
                hT = hTs[bi % 2]
                A = self.colv(self.Acol, i, j, v)
                for t in (range(nt) if tiles is None else tiles):
                    if t >= nt:
                        continue
                    if "A" in part:
                        k = cnt["a"]
                        cnt["a"] += 1
                        xt, xnt, st = xa[k % 2], xn[k % 2], sta[k % 4]
                        s1state[(bi, t)] = xnt
                        r0 = tok0 + t * 128
                        S.op("sp", "dma_start", writes=[xt], stream="ld", out=xt[:], in_=src[r0:r0 + 128, :])
                        S.op("act", "activation", reads=[xt], writes=[junk, st], out=junk[:], in_=xt[:],
                             func=AF.Square, accum_out=st[:, 0:1])
                        self.rstd_from_ss(S, st)
                        S.op("act", "activation", reads=[xt, st], writes=[xnt], out=xnt[:], in_=xt[:],
                             func=AF.Copy, scale=st[:, 2:3])
                    if "B" in part:
                        xnt = s1state.pop((bi, t))
                        pt = YT[cnt["y"] % 2]
                        cnt["y"] += 1
                        for kc in range(8):
                            S.op("pe", "transpose", reads=[xnt, self.ident], writes=[pt],
                                 out=pt[:, kc * 128:(kc + 1) * 128], in_=xnt[:, kc * 128:(kc + 1) * 128],
                                 identity=self.ident[:])
                        for kc in range(8):
                            S.op("dve", "tensor_scalar", reads=[pt, self.Acol, self.mod], writes=[hT],
                                 out=hT[:, kc * 512 + t * 128: kc * 512 + (t + 1) * 128],
                                 in0=pt[:, kc * 128:(kc + 1) * 128], scalar1=A[:, kc:kc + 1],
                                 scalar2=self.modv(i, v, 3 * j)[:, kc:kc + 1], op0=ALU.mult, op1=ALU.add)

            def stage2(bi):
                src, dst, tok0, nt, v = blocks[bi]
                hT = hTs[bi % 2]
                TT = nt * 128
                for fc in range(NFC):
                    if bi + 1 < nb and fc in (0, 5, 10, 15):
                        stage1(bi + 1, [fc // 5], "A")
                    if bi + 1 < nb and fc in (4, 9, 14, 19):
                        stage1(bi + 1, [(fc - 4) // 5], "B")
                    k = bi * 11 + fc // 2
                    if fc % 2 == 0:
                        w1_prefetch(k + NW1 - 1)
                    if fc % 3 == 1:
                        bg.step("act")
                    w = w1[k % NW1]
                    fcl = fc % 2
                    gk = cnt["gu"] % 2
                    cnt["gu"] += 1
                    for half in range(2):
                        ps = GU[gk][half]
                        for kc in range(8):
                            o = kc * 512 + (fcl * 2 + half) * 128
                            S.op("pe", "matmul", reads=[w, hT], writes=[ps], out=ps[:, 0:TT], lhsT=w[:, o:o + 128],
                                 rhs=hT[:, kc * 512: kc * 512 + TT], start=(kc == 0), stop=(kc == 7))
                    S.op("act", "activation", reads=[GU[gk][0]], writes=[sg[gk]], out=sg[gk][:, 0:TT],
                         in_=GU[gk][0][:, 0:TT], func=AF.Silu)
                    S.op("dve", "tensor_tensor", reads=[sg[gk], GU[gk][1]], writes=[actT],
                         out=actT[:, fc * 512: fc * 512 + TT], in0=sg[gk][:, 0:TT], in1=GU[gk][1][:, 0:TT],
                         op=ALU.mult)

            def stage3(bi):
                src, dst, tok0, nt, v = blocks[bi]
                for t in range(nt):
                    k = cnt["st"]
                    cnt["st"] += 1
                    xt, st, yp, tm = xb[k % 2], stb[k % 4], YT[cnt["y"] % 2], tmp[k % 2]
                    cnt["y"] += 1
                    r0 = tok0 + t * 128
                    S.op("sp", "dma_start", writes=[xt], stream="ld", out=xt[:], in_=src[r0:r0 + 128, :])
                    for dh in range(2):
                        for fc in range(NFC):
                            S.op("pe", "matmul", reads=[actT, w2], writes=[yp], out=yp[:, dh * 512:(dh + 1) * 512],
                                 lhsT=actT[:, fc * 512 + t * 128: fc * 512 + (t + 1) * 128],
                                 rhs=w2[:, fc * D + dh * 512: fc * D + (dh + 1) * 512], start=(fc == 0),
                                 stop=(fc == NFC - 1))
                    S.op("act", "activation", reads=[yp], writes=[junk, st], out=junk[:], in_=yp[:],
                         func=AF.Square, accum_out=st[:, 0:1])
                    self.rstd_from_ss(S, st)
                    S.op("dve", "scalar_tensor_tensor", reads=[yp, st, Cbc[v]], writes=[tm], out=tm[:], in0=yp[:],
                         scalar=st[:, 2:3], op0=ALU.mult, in1=Cbc[v][:], op1=ALU.mult)
                    S.op("dve", "tensor_tensor", reads=[tm, xt], writes=[tm], out=tm[:], in0=tm[:], in1=xt[:],
                         op=ALU.add)
                    S.op("sp", "dma_start", reads=[tm], stream="st", out=dst[r0:r0 + 128, :], in_=tm[:])

            w1_prefetch(NW1 - 2)
            S.op("sp", "dma_start", writes=[w2], stream="wt", out=w2[:], in_=self.w2s[i, f])
            stage1(0)
            for bi in range(nb):
                stage2(bi)
                stage3(bi)
            bg.flush("act")
            S.emit_phase(f"ffn{i}{f}")


    def load_fconst(self, S, P):
        fc32 = P.sb([128, 1920], F32, "fc32")
        fcb = P.sb([128, 1920], BF16, "fcb")
        S.op("sp", "dma_start", writes=[fc32], stream="ld", out=fc32[:], in_=self.fconst)
        S.op("dve", "tensor_copy", reads=[fc32], writes=[fcb], out=fcb[:], in_=fc32[:])
        return fcb

    def pre_tokmajor(self, S, src_rows, xt, st, tm, hb, junk, Abc, Bbc):
        for (p0, npart, ap) in (src_rows or []):
            S.op("sp", "dma_start", writes=[xt], stream="ld", out=xt[p0:p0 + npart, :], in_=ap)
        S.op("act", "activation", reads=[xt], writes=[junk, st], out=junk[:], in_=xt[:], func=AF.Square,
             accum_out=st[:, 0:1])
        self.rstd_from_ss(S, st)
        S.op("dve", "scalar_tensor_tensor", reads=[xt, st, Abc], writes=[tm], out=tm[:], in0=xt[:],
             scalar=st[:, 2:3], op0=ALU.mult, in1=Abc[:], op1=ALU.mult)
        S.op("dve", "tensor_tensor", reads=[tm, Bbc], writes=[hb], out=hb[:], in0=tm[:], in1=Bbc[:], op=ALU.add)

    def phase_fourier1(self, S, i):
        nc = self.nc
        j = 1
        with Phase(nc) as P:
            fcb = self.load_fconst(S, P)
            Abc = P.sb([128, D], F32, "Abc")
            Bbc = P.sb([128, D], F32, "Bbc")
            junk = P.sb([128, D], F32, "junk")
            xa = [P.sb([128, D], F32, f"xa{k}") for k in range(3)]
            tm = [P.sb([128, D], F32, f"tm{k}") for k in range(2)]
            hb = [P.sb([128, D], BF16, f"hb{k}") for k in range(2)]
            uvs = [P.sb([128, 2 * D], BF16, f"uvs{k}") for k in range(2)]
            st = [P.sb([128, 4], F32, f"st{k}") for k in range(4)]
            PU = [P.ps([128, D], F32, f"pu{k}") for k in range(2)]
            PV = [P.ps([128, D], F32, f"pv{k}") for k in range(2)]
            self.make_bc(S, P, self.colv(self.Acol, i, j, 0), [self.Acol], Abc, junk, PU[0])
            self.make_bc(S, P, self.modv(i, 0, 3 * j), [self.mod], Bbc, junk, PU[1])
            def f1load(t):
                S.op("sp", "dma_start", writes=[xa[t % 3]], stream="ld", out=xa[t % 3][:],
                     in_=self.y[t * 128:(t + 1) * 128, :])

            f1load(0)
            f1load(1)
            for t in range(32):
                k = t % 2
                if t + 2 < 32:
                    f1load(t + 2)
                self.pre_tokmajor(S, None, xa[t % 3], st[t % 4], tm[k], hb[k], junk, Abc, Bbc)
                for h in range(2):
                    S.op("pe", "matmul", reads=[fcb, hb[k]], writes=[PU[k]], out=PU[k][:, h * 512:(h + 1) * 512],
                         lhsT=fcb[:, 0:128], rhs=hb[k][:, h * 512:(h + 1) * 512], start=True, stop=True)
                for h in range(2):
                    S.op("pe", "matmul", reads=[fcb, hb[k]], writes=[PV[k]], out=PV[k][:, h * 512:(h + 1) * 512],
                         lhsT=fcb[:, 128:256], rhs=hb[k][:, h * 512:(h + 1) * 512], start=True, stop=True)
                S.op("act", "copy", reads=[PU[k]], writes=[uvs[k]], out=uvs[k][:, 0:D], in_=PU[k][:])
                S.op("dve", "tensor_copy", reads=[PV[k]], writes=[uvs[k]], out=uvs[k][:, D:2 * D], in_=PV[k][:])
                S.op("sp", "dma_start", reads=[uvs[k]], stream="st", out=self.uv[t * 128:(t + 1) * 128, :],
                     in_=uvs[k][:])
            S.emit_phase(f"four1_{i}")

    def phase_fourier2(self, S, i, do_ctx):
        nc = self.nc
        j = 1
        jm = i // 2
        with Phase(nc) as P:
            fcb = self.load_fconst(S, P)
            BDc, BDs, nBDs = fcb[:, 0:128], fcb[:, 128:256], fcb[:, 256:384]

            def Ck(kc, kk):
                return fcb[:, 384 + kc * 256 + kk * 128: 384 + kc * 256 + (kk + 1) * 128]

            def Sk(kc, kk):
                return fcb[:, 896 + kc * 256 + kk * 128: 896 + kc * 256 + (kk + 1) * 128]

            def nSk(kc, kk):
                return fcb[:, 1408 + kc * 256 + kk * 128: 1408 + kc * 256 + (kk + 1) * 128]

            wf = P.sb([128, 8 * D], BF16, "wf")
            S.op("sp", "dma_start", writes=[wf], stream="wt", out=wf[:], in_=self.wfs[jm])
            junk = P.sb([128, D], F32, "junk")
            Cbc = [P.sb([128, D], F32, f"Cbc{v}") for v in range(2)]
            uvt = [P.sb([128, 2 * D], BF16, f"uvt{k}") for k in range(3)]
            abT = [P.sb([128, 2 * D], BF16, f"abT{k}") for k in range(2)]
            yTs = [P.sb([128, D], BF16, f"yTs{k}") for k in range(2)]
            xb = [P.sb([128, D], F32, f"xb{k}") for k in range(3)]
            tm = [P.sb([128, D], F32, f"tm{k}") for k in range(2)]
            st = [P.sb([128, 4], F32, f"st{k}") for k in range(4)]
            PA = P.ps([128, D], F32, "pa")
            PB = P.ps([128, D], F32, "pb")
            PY = P.ps([128, D], F32, "py")
            PO = P.ps([128, D], F32, "po")
            nv = 2 if do_ctx else 1
            for v in range(nv):
                self.make_bc(S, P, self.colv(self.Ccol, i, j, v), [self.Ccol], Cbc[v], junk, PO)
            cnt = {"k": 0}

            def tile_proc(termsA, termsB, term_reads, res_rows, v, preloaded=False, xbt=None):
                k = cnt["k"] % 2
                cnt["k"] += 1
                for (terms, PX) in ((termsA, PA), (termsB, PB)):
                    for dc in range(8):
                        for ti, (lh, rh) in enumerate(terms):
                            S.op("pe", "matmul", reads=[fcb] + term_reads, writes=[PX],
                                 out=PX[:, dc * 128:(dc + 1) * 128], lhsT=lh(dc), rhs=rh, start=(ti == 0),
                                 stop=(ti == len(terms) - 1))
                S.op("act", "copy", reads=[PA], writes=[abT[k]], out=abT[k][:, 0:D], in_=PA[:])
                S.op("dve", "tensor_copy", reads=[PB], writes=[abT[k]], out=abT[k][:, D:2 * D], in_=PB[:])
                for g in range(4):
                    for kk in range(2):
                        oc = (g * 2 + kk) * 128
                        n = 0
                        for kc in range(2):
                            ic = (g * 2 + kc) * 128
                            S.op("pe", "matmul", reads=[fcb, abT[k]], writes=[PY], out=PY[:, oc:oc + 128],
                                 lhsT=Ck(kc, kk), rhs=abT[k][:, ic:ic + 128], start=(n == 0), stop=False)
                            n += 1
                            S.op("pe", "matmul", reads=[fcb, abT[k]], writes=[PY], out=PY[:, oc:oc + 128],
                                 lhsT=nSk(kc, kk), rhs=abT[k][:, D + ic:D + ic + 128], start=False, stop=(n == 3))
                            n += 1
                S.op("act", "copy", reads=[PY], writes=[yTs[k]], out=yTs[k][:], in_=PY[:])
                for dh in range(2):
                    for kc in range(8):
                        S.op("pe", "matmul", reads=[yTs[k], wf], writes=[PO], out=PO[:, dh * 512:(dh + 1) * 512],
                             lhsT=yTs[k][:, kc * 128:(kc + 1) * 128],
                             rhs=wf[:, kc * D + dh * 512: kc * D + (dh + 1) * 512], start=(kc == 0), stop=(kc == 7))
                xt, stt, tmm = (xbt if xbt is not None else xb[k]), st[cnt["k"] % 4], tm[k]
                if not preloaded:
                    for (p0, npart, ap) in res_rows:
                        S.op("sp", "dma_start", writes=[xt], stream="ld", out=xt[p0:p0 + npart, :], in_=ap)
                S.op("act", "activation", reads=[PO], writes=[junk, stt], out=junk[:], in_=PO[:], func=AF.Square,
                     accum_out=stt[:, 0:1])
                self.rstd_from_ss(S, stt)
                S.op("dve", "scalar_tensor_tensor", reads=[PO, stt, Cbc[v]], writes=[tmm], out=tmm[:], in0=PO[:],
                     scalar=stt[:, 2:3], op0=ALU.mult, in1=Cbc[v][:], op1=ALU.mult)
                S.op("dve", "tensor_tensor", reads=[tmm, xt], writes=[tmm], out=tmm[:], in0=tmm[:], in1=xt[:],
                     op=ALU.add)
                for (p0, npart, ap) in res_rows:
                    S.op("sp", "dma_start", reads=[tmm], stream="st", out=ap, in_=tmm[p0:p0 + npart, :])

            if do_ctx:
                Abc = P.sb([128, D], F32, "Abc")
                Bbc = P.sb([128, D], F32, "Bbc")
                self.make_bc(S, P, self.colv(self.Acol, i, j, 1), [self.Acol], Abc, junk, PO)
                self.make_bc(S, P, self.modv(i, 1, 3 * j), [self.mod], Bbc, junk, PO)
                hc = [P.sb([128, D], BF16, f"hc{k}") for k in range(2)]
                xa = P.sb([128, D], F32, "xa")
                for lt in range(2):
                    self.pre_tokmajor(S, [(0, 128, self.cs[lt * 128:(lt + 1) * 128, :])], xa, st[lt], tm[lt], hc[lt],
                                      junk, Abc, Bbc)
                for hh in range(2):
                    tA = [((lambda dc, lt=lt: hc[lt][:, dc * 128:(dc + 1) * 128]), Ck(lt, hh)) for lt in range(2)]
                    tB = [((lambda dc, lt=lt: hc[lt][:, dc * 128:(dc + 1) * 128]), Sk(lt, hh)) for lt in range(2)]
                    tile_proc(tA, tB, [hc[0], hc[1]], [(0, 128, self.cs[hh * 128:(hh + 1) * 128, :])], 1)

            yv = self.y.rearrange("(r c) f -> c r f", c=64)
            uvv = self.uv.rearrange("(r c) f -> c r f", c=64)
            kbase = cnt["k"]

            def f2load(u):
                ut = uvt[u % 3]
                xt = xb[u % 3]
                for cb in range(2):
                    S.op("sp", "dma_start", writes=[ut], stream="ld", out=ut[cb * 64:(cb + 1) * 64, :],
                         in_=uvv[2 * u + cb])
                for cb in range(2):
                    S.op("sp", "dma_start", writes=[xt], stream="ld", out=xt[cb * 64:(cb + 1) * 64, :],
                         in_=yv[2 * u + cb])

            f2load(0)
            f2load(1)
            for u in range(32):
                ut = uvt[u % 3]
                if u + 2 < 32:
                    f2load(u + 2)
                U = lambda dc, ut=ut: ut[:, dc * 128:(dc + 1) * 128]
                V = lambda dc, ut=ut: ut[:, D + dc * 128: D + (dc + 1) * 128]
                tile_proc([(U, BDc), (V, nBDs)], [(V, BDc), (U, BDs)], [ut],
                          [(cb * 64, 64, yv[2 * u + cb]) for cb in range(2)], 0, preloaded=True, xbt=xb[u % 3])
            S.emit_phase(f"four2_{i}")

    def tile_rows(self, tt):
        return self.cs[tt * 128:(tt + 1) * 128, :] if tt < 2 else self.y[(tt - 2) * 128:(tt - 1) * 128, :]

    def phase_h1(self, S, i):
        nc = self.nc
        j = 1
        jm = i // 2
        with Phase(nc) as P:
            hc = P.sb([128, 776], F32, "hc")
            S.op("sp", "dma_start", writes=[hc], stream="ld", out=hc[:], in_=self.hconst)
            hcb = P.sb([128, 264], BF16, "hcb")
            S.op("dve", "tensor_copy", reads=[hc], writes=[hcb], out=hcb[:], in_=hc[:, 0:264])
            Dm = [hcb[:, 0:128], hcb[:, 132:260]]
            DmD = [hcb[:, 0:132], hcb[:, 132:264]]
            wh = P.sb([128, 8 * 5 * D], BF16, "wh")
            for kc in range(8):
                S.op("sp", "dma_start", writes=[wh], stream="wt", out=wh[:, kc * 5120:(kc + 1) * 5120],
                     in_=self.wh1s[jm][:, kc * 5120:(kc + 1) * 5120])
            junk = P.sb([128, D], F32, "junk")
            PT = P.ps([128, D], F32, "PT")
            banks = [P.ps([128, 512], F32, f"bk{k}") for k in range(6)]
            bstate = {"n": 0}

            def nb():
                bstate["n"] += 1
                return banks[bstate["n"] % 6]

            lb_bc = oml_bc = None
            if jm == 1:
                lbc = P.sb([128, 32], F32, "lbc")
                lbd = P.sb([128, 16], F32, "lbd")
                oml = P.sb([128, 16], F32, "oml")
                S.op("sp", "dma_start", writes=[lbc], stream="ld", out=lbc[:], in_=self.lbcol)
                lv = lbc[:, :].rearrange("p (d l k) -> p d l k", d=2, l=2)
                S.op("dve", "tensor_tensor", reads=[lbc], writes=[lbd], out=lbd[:, :].rearrange("p (d k) -> p d k", d=2),
                     in0=lv[:, :, 1, :], in1=lv[:, :, 0, :], op=ALU.subtract)
                S.op("act", "activation", reads=[lbd], writes=[lbd], out=lbd[:], in_=lbd[:], func=AF.Sigmoid)
                S.op("dve", "tensor_scalar", reads=[lbd], writes=[oml], out=oml[:], in0=lbd[:], scalar1=-1.0,
                     scalar2=1.0, op0=ALU.mult, op1=ALU.add)
                lb_bc = [P.sb([128, D], F32, f"lbbc{d}") for d in range(2)]
                oml_bc = [P.sb([128, D], F32, f"omlbc{d}") for d in range(2)]
                for d in range(2):
                    self.make_bc(S, P, lbd[:, d * 8:(d + 1) * 8], [lbd], lb_bc[d], junk, PT)
                    self.make_bc(S, P, oml[:, d * 8:(d + 1) * 8], [oml], oml_bc[d], junk, PT)
            xa = [P.sb([128, D], F32, f"xa{k}") for k in range(2)]
            xn = [P.sb([128, D], F32, "xn0")] * 2
            sta = [P.sb([128, 4], F32, f"sta{k}") for k in range(4)]
            hT = [P.sb([128, D], BF16, f"hT{k}") for k in range(2)]
            qT32 = [P.sb([128, D], F32, f"qT32{k}") for k in range(2)]
            vs = [P.sb([128, D], BF16, f"vs{k}") for k in range(2)]
            sgs = [P.sb([128, D], F32, f"sgs{k}") for k in range(2)]
            ft = [[P.sb([128, D], F32, f"ft{k}{d}") for d in range(2)] for k in range(2)]
            lt = [P.sb([128, D], F32, "lt0")] * 2
            lhi = [P.sb([128, D], BF16, f"lhi{d}") for d in range(2)]
            llo = [P.sb([128, D], BF16, f"llo{d}") for d in range(2)]
            ent = [P.sb([128, D], F32, f"ent{d}") for d in range(2)]
            eqx = [P.sb([128, 8 * 132], F32, f"eqx{d}") for d in range(2)]
            ktb = [P.sb([128, D], BF16, f"ktb{d}") for d in range(2)]
            kTb = [P.sb([128, D], BF16, f"kTb{d}") for d in range(2)]
            qtb = [P.sb([128, D], BF16, f"qtb{d}") for d in range(2)]
            dect = [P.sb([128, 64], F32, f"dect{k}") for k in range(2)]

            def stageA(tt, part="AB"):
                v = 1 if tt < 2 else 0
                A = self.colv(self.Acol, i, j, v)
                k = tt % 2
                xt, xnt, st = xa[k], xn[k], sta[tt % 4]
                if "A" in part:
                    S.op("sp", "dma_start", writes=[xt], stream="ld", out=xt[:], in_=self.tile_rows(tt))
                    S.op("act", "activation", reads=[xt], writes=[junk, st], out=junk[:], in_=xt[:], func=AF.Square,
                         accum_out=st[:, 0:1])
                    self.rstd_from_ss(S, st)
                    S.op("act", "activation", reads=[xt, st], writes=[xnt], out=xnt[:], in_=xt[:], func=AF.Copy,
                         scale=st[:, 2:3])
                if "B" in part:
                    for kc in range(8):
                        S.op("pe", "transpose", reads=[xnt, self.ident], writes=[PT], out=PT[:, kc * 128:(kc + 1) * 128],
                             in_=xnt[:, kc * 128:(kc + 1) * 128], identity=self.ident[:])
                    for kc in range(8):
                        S.op("dve", "tensor_scalar", reads=[PT, self.Acol, self.mod], writes=[hT[k]],
                             out=hT[k][:, kc * 128:(kc + 1) * 128], in0=PT[:, kc * 128:(kc + 1) * 128],
                             scalar1=A[:, kc:kc + 1], scalar2=self.modv(i, v, 3 * j)[:, kc:kc + 1], op0=ALU.mult,
                             op1=ALU.add)

            def projq(tt, p):
                k = tt % 2
                bk = nb()
                for hh in range(4):
                    h = p * 4 + hh
                    for kc in range(8):
                        S.op("pe", "matmul", reads=[wh, hT[k]], writes=[bk], out=bk[:, hh * 128:(hh + 1) * 128],
                             lhsT=wh[:, kc * 5120 + h * 128: kc * 5120 + (h + 1) * 128],
                             rhs=hT[k][:, kc * 128:(kc + 1) * 128], start=(kc == 0), stop=(kc == 7))
                S.op("act", "activation", reads=[bk], writes=[qT32[k]], out=qT32[k][:, p * 512:(p + 1) * 512],
                     in_=bk[:], func=AF.Silu)

            def projtok(tt, c0, p, func, dst, iscopy=False):
                k = tt % 2
                bk = nb()
                for kc in range(8):
                    o = kc * 5120 + c0 + p * 512
                    S.op("pe", "matmul", reads=[wh, hT[k]], writes=[bk], out=bk[:], lhsT=hT[k][:, kc * 128:(kc + 1) * 128],
                         rhs=wh[:, o:o + 512], start=(kc == 0), stop=(kc == 7))
                if iscopy:
                    S.op("act", "copy", reads=[bk], writes=[dst], out=dst[:, p * 512:(p + 1) * 512], in_=bk[:])
                else:
                    S.op("act", "activation", reads=[bk], writes=[dst], out=dst[:, p * 512:(p + 1) * 512], in_=bk[:],
                         func=func)

            def stageB_parts(tt):
                k = tt % 2
                parts = []
                parts.append(lambda: [projq(tt, p) for p in range(2)])

                def vg():
                    for p in range(2):
                        projtok(tt, 1024, p, None, vs[k], iscopy=True)
                    S.op("sp", "dma_start", reads=[vs[k]], stream="st", out=self.hv[tt], in_=vs[k][:])
                    for p in range(2):
                        projtok(tt, 4096, p, AF.Silu, sgs[k])
                    S.op("sp", "dma_start", reads=[sgs[k]], stream="st", out=self.hsg[tt], in_=sgs[k][:])
                parts.append(vg)

                def zz():
                    for d in range(2):
                        for p in range(2):
                            projtok(tt, 2048 + d * 1024, p, AF.Sigmoid, ft[k][d])
                        if jm == 1:
                            f = ft[k][d]
                            S.op("dve", "tensor_tensor", reads=[f, oml_bc[d]], writes=[f], out=f[:], in0=f[:],
                                 in1=oml_bc[d][:], op=ALU.mult)
                            S.op("dve", "tensor_tensor", reads=[f, lb_bc[d]], writes=[f], out=f[:], in0=f[:],
                                 in1=lb_bc[d][:], op=ALU.add)
                parts.append(zz)
                return parts

            def stageC_parts(tt):
                k = tt % 2
                parts = []

                def c1():
                    for d in range(2):
                        S.op("act", "activation", reads=[ft[k][d]], writes=[lt[d]], out=lt[d][:], in_=ft[k][d][:],
                             func=AF.Ln)
                        S.op("dve", "tensor_copy", reads=[lt[d]], writes=[lhi[d]], out=lhi[d][:], in_=lt[d][:])
                        S.op("dve", "tensor_tensor", reads=[lt[d], lhi[d]], writes=[llo[d]], out=llo[d][:], in0=lt[d][:],
                             in1=lhi[d][:], op=ALU.subtract)
                    kt = ft[k]
                    for d in range(2):
                        S.op("dve", "tensor_scalar", reads=[ft[k][d]], writes=[kt[d]], out=kt[d][:], in0=ft[k][d][:],
                             scalar1=-1.0, scalar2=1.0, op0=ALU.mult, op1=ALU.add)
                    for d in range(2):
                        for p in range(2):
                            bk = nb()
                            S.op("pe", "matmul", reads=[hcb, lhi[d]], writes=[bk], out=bk[:], lhsT=Dm[d],
                                 rhs=lhi[d][:, p * 512:(p + 1) * 512], start=True, stop=False)
                            S.op("pe", "matmul", reads=[hcb, llo[d]], writes=[bk], out=bk[:], lhsT=Dm[d],
                                 rhs=llo[d][:, p * 512:(p + 1) * 512], start=False, stop=True)
                            S.op("act", "activation", reads=[bk], writes=[ent[d]], out=ent[d][:, p * 512:(p + 1) * 512],
                                 in_=bk[:], func=AF.Exp, scale=-1.0)
                parts.append(c1)

                def c2():
                    kt = ft[k]
                    for d in range(2):
                        S.op("dve", "tensor_tensor", reads=[kt[d], ent[d]], writes=[ktb[d]], out=ktb[d][:], in0=kt[d][:],
                             in1=ent[d][:], op=ALU.mult)
                        S.op("sp", "dma_start", reads=[ktb[d]], stream="st", out=self.hk[d][tt], in_=ktb[d][:])
                    for d in range(2):
                        for (h0, nh) in ((0, 3), (3, 3), (6, 2)):
                            bk = nb()
                            for hh in range(nh):
                                h = h0 + hh
                                S.op("pe", "matmul", reads=[hcb, lhi[d]], writes=[bk], out=bk[:, hh * 132:(hh + 1) * 132],
                                     lhsT=lhi[d][:, h * 128:(h + 1) * 128], rhs=DmD[d], start=True, stop=False)
                                S.op("pe", "matmul", reads=[hcb, llo[d]], writes=[bk], out=bk[:, hh * 132:(hh + 1) * 132],
                                     lhsT=llo[d][:, h * 128:(h + 1) * 128], rhs=DmD[d], start=False, stop=True)
                            S.op("act", "activation", reads=[bk], writes=[eqx[d]],
                                 out=eqx[d][:, h0 * 132:(h0 + nh) * 132], in_=bk[:, 0:nh * 132], func=AF.Exp)
                    for d in range(2):
                        S.op("dve", "tensor_copy", reads=[eqx[d]], writes=[dect[k]],
                             out=dect[k][:, d * 32:(d + 1) * 32].rearrange("p (h e) -> p h e", h=8),
                             in_=eqx[d][:, :].rearrange("p (h e) -> p h e", h=8)[:, :, 128:132])
                    S.op("sp", "dma_start", reads=[dect[k]], stream="st", out=self.hdec[tt], in_=dect[k][:])
                parts.append(c2)

                def c3():
                    for d in range(2):
                        S.op("dve", "tensor_tensor", reads=[qT32[k], eqx[d]], writes=[qtb[d]],
                             out=qtb[d][:, :].rearrange("p (h t) -> p h t", h=8),
                             in0=qT32[k][:, :].rearrange("p (h t) -> p h t", h=8),
                             in1=eqx[d][:, :].rearrange("p (h e) -> p h e", h=8)[:, :, 0:128], op=ALU.mult)
                        S.op("sp", "dma_start", reads=[qtb[d]], stream="st", out=self.hq[d][tt], in_=qtb[d][:])
                    for d in range(2):
                        bk = nb()
                        pk = bk[:, :].bitcast(BF16)
                        for h in range(8):
                            S.op("pe", "transpose", reads=[ktb[d], self.identb], writes=[bk],
                                 out=pk[:, h * 128:(h + 1) * 128], in_=ktb[d][:, h * 128:(h + 1) * 128],
                                 identity=self.identb[:])
                        S.op("dve", "tensor_copy", reads=[bk], writes=[kTb[d]], out=kTb[d][:], in_=pk)
                        S.op("sp", "dma_start", reads=[kTb[d]], stream="st", out=self.hkT[d][tt], in_=kTb[d][:])
                parts.append(c3)
                return parts

            NTL = 34
            stageA(0)
            for f_ in stageB_parts(0):
                f_()
            stageA(1, "A")
            for tt in range(NTL):
                if tt + 1 < NTL:
                    stageA(tt + 1, "B")
                cp = stageC_parts(tt)
                bp = stageB_parts(tt + 1) if tt + 1 < NTL else []
                for n in range(3):
                    cp[n]()
                    if n == 1 and tt + 2 < NTL:
                        stageA(tt + 2, "A")
                    if n < len(bp):
                        bp[n]()
            S.emit_phase(f"h1_{i}")

    def phase_hscan(self, S, i, dirs=(0, 1)):
        nc = self.nc
        with Phase(nc) as P:
            hc = P.sb([128, 776], F32, "hc")
            S.op("sp", "dma_start", writes=[hc], stream="ld", out=hc[:], in_=self.hconst)
            gens = [self.hscan_gen(S, P, i, d, hc) for d in dirs]
            while gens:
                for g_ in list(gens):
                    try:
                        next(g_)
                    except StopIteration:
                        gens.remove(g_)
            S.emit_phase(f"hscan{i}")

    def hscan_gen(self, S, P, i, d, hc):
        Um = hc[:, 264 + d * 256: 264 + d * 256 + 128]
        Lm = hc[:, 392 + d * 256: 392 + d * 256 + 128]
        PAT = P.ps([128, 512], F32, f"PAT{d}")
        PO = P.ps([128, 512], F32, f"PO{d}")
        PI = P.ps([128, 512], F32, f"PI{d}")
        PM = P.ps([128, 512], F32, f"PM{d}")
        qT = [P.sb([128, D], BF16, f"qT{d}{k}") for k in range(2)]
        kT = [P.sb([128, D], BF16, f"kT{d}{k}") for k in range(2)]
        kk = [P.sb([128, D], BF16, f"kk{d}{k}") for k in range(2)]
        vv = [P.sb([128, D], BF16, f"vv{d}{k}") for k in range(2)]
        dec = [P.sb([128, 64], F32, f"dec{d}{k}") for k in range(2)]
        W = [P.sb([128, 128], F32, f"W{d}{h}") for h in range(8)]
        Sb = [P.sb([128, 128], BF16, f"Sb{d}{h}") for h in range(8)]
        g = [[P.sb([128, 8], F32, f"g{d}{k}{c}") for c in range(2)] for k in range(2)]
        endp = [P.sb([128, 8], F32, f"endp{d}{k}") for k in range(2)]
        ATs = [P.sb([128, 512], BF16, f"ATs{d}{k}") for k in range(2)]
        pat_sb = [P.sb([128, 512], F32, f"pat{d}{k}") for k in range(2)]
        osb = [P.sb([128, D], F32, f"osb{d}{k}") for k in range(2)]
        for h in range(8):
            S.op("dve", "memset", writes=[W[h]], ap=W[h][:], constant=0.0)
        order = list(range(34)) if d == 0 else [1, 0] + list(range(33, 1, -1))
        chunks = (0, 1) if d == 0 else (1, 0)
        prev_end = None
        na = 0

        def loads(ti):
            tt = order[ti]
            k = ti % 2
            S.op("sp", "dma_start", writes=[qT[k]], stream="ld", out=qT[k][:], in_=self.hq[d][tt])
            S.op("sp", "dma_start", writes=[kT[k]], stream="ld", out=kT[k][:], in_=self.hkT[d][tt])
            S.op("sp", "dma_start", writes=[kk[k]], stream="ld", out=kk[k][:], in_=self.hk[d][tt])
            S.op("sp", "dma_start", writes=[vv[k]], stream="ld", out=vv[k][:], in_=self.hv[tt])
            S.op("sp", "dma_start", writes=[dec[k]], stream="ld", out=dec[k][:], in_=self.hdec[tt])

        loads(0)
        for ti, tt in enumerate(order):
            k = ti % 2
            if ti + 1 < len(order):
                loads(ti + 1)
            dv = dec[k][:, d * 32:(d + 1) * 32].rearrange("p (h c e) -> p h c e", h=8, c=2)
            for ci, c in enumerate(chunks):
                gq = g[k][ci]
                if prev_end is None:
                    S.op("dve", "tensor_copy", reads=[dec[k]], writes=[gq], out=gq[:], in_=dv[:, :, c, 0])
                    prev_end = endp[0]
                elif ci == 0:
                    S.op("dve", "tensor_tensor", reads=[dec[k], prev_end], writes=[gq], out=gq[:],
                         in0=dv[:, :, c, 0], in1=prev_end[:], op=ALU.mult)
                else:
                    S.op("dve", "tensor_tensor", reads=[dec[k]], writes=[gq], out=gq[:], in0=dv[:, :, c, 0],
                         in1=dv[:, :, chunks[0], 1], op=ALU.mult)
            prev_end = endp[ti % 2]
            S.op("dve", "tensor_copy", reads=[dec[k]], writes=[prev_end], out=prev_end[:], in_=dv[:, :, chunks[1], 1])
            for hg in range(2):
                at = ATs[na % 2]
                na += 1
                for hh in range(4):
                    h = hg * 4 + hh
                    S.op("pe", "matmul", reads=[kT[k], qT[k]], writes=[PAT], out=PAT[:, hh * 128:(hh + 1) * 128],
                         lhsT=kT[k][:, h * 128:(h + 1) * 128], rhs=qT[k][:, h * 128:(h + 1) * 128], start=True,
                         stop=True)
                yield
                psb = pat_sb[na % 2]
                S.op("act", "copy", reads=[PAT], writes=[psb], out=psb[:], in_=PAT[:])
                S.op("pool", "affine_select", reads=[psb], writes=[at],
                     out=at[:, :].rearrange("p (h t) -> p h t", h=4), in_=psb[:, :].rearrange("p (h t) -> p h t", h=4),
                     pattern=[[0, 4], [1 if d == 0 else -1, 128]], compare_op=ALU.is_ge, fill=0.0, base=0,
                     channel_multiplier=(-1 if d == 0 else 1))
                if d == 0:
                    zb = at[0:64, :].rearrange("p (h t) -> p h t", h=4)[:, :, 64:128]
                else:
                    zb = at[64:128, :].rearrange("p (h t) -> p h t", h=4)[:, :, 0:64]
                S.op("pool", "memset", writes=[at], ap=zb, constant=0.0)
                yield
                for hh in range(4):
                    h = hg * 4 + hh
                    S.op("pe", "matmul", reads=[at, vv[k]], writes=[PO], out=PO[:, hh * 128:(hh + 1) * 128],
                         lhsT=at[:, hh * 128:(hh + 1) * 128], rhs=vv[k][:, h * 128:(h + 1) * 128],
                         start=True, stop=True)
                for ci, c in enumerate(chunks):
                    gq = g[k][ci]
                    for hh in range(4):
                        h = hg * 4 + hh
                        S.op("pe", "matmul", reads=[kk[k], vv[k]], writes=[PM], out=PM[:, hh * 128:(hh + 1) * 128],
                             lhsT=kk[k][c * 64:(c + 1) * 64, h * 128:(h + 1) * 128],
                             rhs=vv[k][c * 64:(c + 1) * 64, h * 128:(h + 1) * 128], start=True, stop=True)
                    for hh in range(4):
                        h = hg * 4 + hh
                        if hh % 2 == 0:
                            S.op("act", "activation", reads=[W[h], gq], writes=[Sb[h]], out=Sb[h][:], in_=W[h][:],
                                 func=AF.Copy, scale=gq[:, h:h + 1])
                        else:
                            S.op("pool", "tensor_scalar", reads=[W[h], gq], writes=[Sb[h]], out=Sb[h][:], in0=W[h][:],
                                 scalar1=gq[:, h:h + 1], scalar2=1.0, op0=ALU.mult, op1=ALU.mult)
                        S.op("pe", "matmul", reads=[qT[k], Sb[h]], writes=[PI],
                             out=PI[c * 64:(c + 1) * 64, hh * 128:(hh + 1) * 128],
                             lhsT=qT[k][:, h * 128 + c * 64: h * 128 + (c + 1) * 64], rhs=Sb[h][:], start=True,
                             stop=True)
                        S.op("dve", "scalar_tensor_tensor", reads=[W[h], gq, PM], writes=[W[h]], out=W[h][:],
                             in0=W[h][:], scalar=gq[:, h:h + 1], op0=ALU.mult, in1=PM[:, hh * 128:(hh + 1) * 128],
                             op1=ALU.add)
                    yield
                S.op("act", "copy", reads=[PO], writes=[osb[k]], out=osb[k][:, hg * 512:(hg + 1) * 512], in_=PO[:])
                S.op("dve", "tensor_tensor", reads=[PI, osb[k]], writes=[osb[k]], out=osb[k][:, hg * 512:(hg + 1) * 512],
                     in0=PI[:], in1=osb[k][:, hg * 512:(hg + 1) * 512], op=ALU.add)
                yield
            S.op("sp", "dma_start", reads=[osb[k]], stream="st", out=self.ho[d][tt], in_=osb[k][:])

    def phase_h4(self, S, i, do_ctx):
        nc = self.nc
        j = 1
        jm = i // 2
        with Phase(nc) as P:
            who = P.sb([128, 8 * D], BF16, "who")
            S.op("sp", "dma_start", writes=[who], stream="wt", out=who[:], in_=self.whos[jm])
            junk = P.sb([128, D], F32, "junk")
            PY = P.ps([128, D], F32, "PY")
            PK = P.ps([128, D], BF16, "PK")
            PG = P.ps([128, 128], F32, "PG")
            Cbc = [P.sb([128, D], F32, f"Cbc{v}") for v in range(2)]
            for v in range(2 if do_ctx else 1):
                self.make_bc(S, P, self.colv(self.Ccol, i, j, v), [self.Ccol], Cbc[v], junk, PY)
            gnc = P.sb([128, 2], F32, "gnc")
            gnbc = P.sb([128, 128], F32, "gnbc")
            S.op("sp", "dma_start", writes=[gnc], stream="ld", out=gnc[:], in_=self.gncol)
            S.op("dve", "tensor_scalar", reads=[self.ident, gnc], writes=[junk], out=junk[:, 0:128], in0=self.ident[:],
                 scalar1=gnc[:, jm:jm + 1], scalar2=None, op0=ALU.mult)
            S.op("pe", "matmul", reads=[self.ones, junk], writes=[PG], out=PG[:], lhsT=self.ones[:], rhs=junk[:, 0:128],
                 start=True, stop=True)
            S.op("act", "copy", reads=[PG], writes=[gnbc], out=gnbc[:], in_=PG[:])
            of = [P.sb([128, D], F32, f"of{k}") for k in range(3)]
            ob = [P.sb([128, D], F32, f"ob{k}") for k in range(3)]
            sg = [P.sb([128, D], F32, f"sg{k}") for k in range(3)]
            xb = [P.sb([128, D], F32, f"xb{k}") for k in range(3)]
            sq = [P.sb([128, D], F32, f"sq{k}") for k in range(2)]
            on = [P.sb([128, D], F32, f"on{k}") for k in range(2)]
            og = [P.sb([128, D], BF16, f"og{k}") for k in range(2)]
            ogT = [P.sb([128, D], BF16, f"ogT{k}") for k in range(2)]
            tm = [P.sb([128, D], F32, f"tm{k}") for k in range(2)]
            s8 = [P.sb([128, 24], F32, f"s8{k}") for k in range(2)]
            st = [P.sb([128, 4], F32, f"st{k}") for k in range(4)]
            tiles = list(range(0 if do_ctx else 2, 34))
            onb = [[Buf(f"on{k}{h}") for h in range(8)] for k in range(2)]

            def h4loads(ti):
                tt = tiles[ti]
                m = ti % 3
                S.op("sp", "dma_start", writes=[of[m]], stream="ld", out=of[m][:], in_=self.ho[0][tt])
                S.op("sp", "dma_start", writes=[ob[m]], stream="ld", out=ob[m][:], in_=self.ho[1][tt])
                S.op("sp", "dma_start", writes=[sg[m]], stream="ld", out=sg[m][:], in_=self.hsg[tt])
                S.op("sp", "dma_start", writes=[xb[m]], stream="ld", out=xb[m][:], in_=self.tile_rows(tt))

            def front(ti):
                tt = tiles[ti]
                k = ti % 2
                m = ti % 3
                S.op("dve", "tensor_tensor", reads=[of[m], ob[m]], writes=[of[m]], out=of[m][:], in0=of[m][:],
                     in1=ob[m][:], op=ALU.add)
                S.op("act", "activation", reads=[of[m]], writes=[sq[k]], out=sq[k][:], in_=of[m][:], func=AF.Square)
                s = s8[k]
                S.op("dve", "tensor_reduce", reads=[sq[k]], writes=[s], out=s[:, 0:8],
                     in_=sq[k][:, :].rearrange("p (h v) -> p h v", h=8), op=ALU.add, axis=mybir.AxisListType.X)
                S.op("act", "activation", reads=[s, self.epsc], writes=[s], out=s[:, 8:16], in_=s[:, 0:8], func=AF.Sqrt,
                     scale=1.0 / 128, bias=self.epsc[:, 0:1])
                S.op("dve", "reciprocal", reads=[s], writes=[s], out=s[:, 16:24], in_=s[:, 8:16])
                for h in range(8):
                    S.op("dve", "scalar_tensor_tensor", reads=[of[m], s, gnbc], writes=[onb[k][h]],
                         out=on[k][:, h * 128:(h + 1) * 128], in0=of[m][:, h * 128:(h + 1) * 128],
                         scalar=s[:, 16 + h:17 + h], op0=ALU.mult, in1=gnbc[:], op1=ALU.mult)
                S.op("dve", "tensor_tensor", reads=onb[k] + [sg[m]], writes=[og[k]], out=og[k][:], in0=on[k][:],
                     in1=sg[m][:], op=ALU.mult)

            def back(ti):
                tt = tiles[ti]
                k = ti % 2
                v = 1 if tt < 2 else 0
                rows = self.tile_rows(tt)
                for kc in range(8):
                    S.op("pe", "transpose", reads=[og[k], self.identb], writes=[PK], out=PK[:, kc * 128:(kc + 1) * 128],
                         in_=og[k][:, kc * 128:(kc + 1) * 128], identity=self.identb[:])
                S.op("act", "copy", reads=[PK], writes=[ogT[k]], out=ogT[k][:], in_=PK[:])
                for dh in range(2):
                    for kc in range(8):
                        S.op("pe", "matmul", reads=[ogT[k], who], writes=[PY], out=PY[:, dh * 512:(dh + 1) * 512],
                             lhsT=ogT[k][:, kc * 128:(kc + 1) * 128],
                             rhs=who[:, kc * D + dh * 512: kc * D + (dh + 1) * 512], start=(kc == 0), stop=(kc == 7))
                stt = st[ti % 4]
                S.op("act", "activation", reads=[PY], writes=[junk, stt], out=junk[:], in_=PY[:], func=AF.Square,
                     accum_out=stt[:, 0:1])
                self.rstd_from_ss(S, stt)
                S.op("dve", "scalar_tensor_tensor", reads=[PY, stt, Cbc[v]], writes=[tm[k]], out=tm[k][:], in0=PY[:],
                     scalar=stt[:, 2:3], op0=ALU.mult, in1=Cbc[v][:], op1=ALU.mult)
                S.op("dve", "tensor_tensor", reads=[tm[k], xb[ti % 3]], writes=[tm[k]], out=tm[k][:], in0=tm[k][:],
                     in1=xb[ti % 3][:], op=ALU.add)
                S.op("sp", "dma_start", reads=[tm[k]], stream="st", out=rows, in_=tm[k][:])

            h4loads(0)
            h4loads(1)
            front(0)
            for ti in range(len(tiles)):
                if ti + 2 < len(tiles):
                    h4loads(ti + 2)
                if ti + 1 < len(tiles):
                    front(ti + 1)
                back(ti)
            S.emit_phase(f"h4_{i}")


def build(stop_after=None, layers=(0, 1, 2, 3)):
    nc = bass.Bass("TRN2", target_bir_lowering=False)
    pr = Prog(nc, stop_after)
    with contextlib.ExitStack() as es:
        S = Sched(nc, es)
        pr.alloc_persistent(es)
        with nc.Block() as block:
            @block.sync
            def _(e):
                for k in S.keys:
                    e.sem_clear(S.sem[k])
        pr.phase_init(S)
        pr.make_jobs(layers)
        pr.phase_prologue(S, layers)
        for i in layers:
            first = i == layers[0]
            last = i == DEPTH - 1
            pr.ensure_unit(S, ("F", i, 0))
            pr.phase_ffn(S, i, 0, True, pr.x if first else pr.y, pr.ctx if first else pr.cs)
            if stop_after == ("ffn", i, 0):
                break
            pr.ensure_unit(S, ("M", i))
            if i % 2 == 0:
                pr.phase_fourier1(S, i)
                pr.phase_fourier2(S, i, not last)
            else:
                pr.phase_h1(S, i)
                if stop_after == ("h1", i):
                    break
                pr.phase_hscan(S, i)
                if stop_after == ("hs1", i):
                    break
                pr.phase_h4(S, i, not last)
            if stop_after == ("mix", i):
                break
            pr.ensure_unit(S, ("F", i, 1))
            pr.phase_ffn(S, i, 1, not last, pr.y)
    return nc


def host_inputs(b, inp):
    f = np.float32
    m = {}
    m["x"] = np.ascontiguousarray(inp["x"][b])
    m["ctx"] = np.ascontiguousarray(inp["ctx"][b])
    cc = np.stack([inp["c"][b].reshape(8, 128).T, inp["c_ctx"].reshape(8, 128).T], axis=-1)
    m["ccol"] = np.ascontiguousarray(cc.reshape(128, 16)).astype(f)
    m["ada_w"] = inp["ada_w"]
    m["ada_b_col"] = np.ascontiguousarray(inp["ada_b"].reshape(DEPTH, 72, 128).transpose(2, 0, 1).reshape(128, -1))
    m["npre_col"] = np.ascontiguousarray(inp["norm_pre"].reshape(DEPTH, 3, 8, 128).transpose(3, 0, 1, 2).reshape(128, -1))
    m["npost_col"] = np.ascontiguousarray(inp["norm_post"].reshape(DEPTH, 3, 8, 128).transpose(3, 0, 1, 2).reshape(128, -1))
    return m


_SHARED = {}


def host_shared(inp):
    w = inp["ffn_w_in"]
    idx = (np.arange(2)[None, :, None] * DFF + np.arange(NFC)[:, None, None] * 128 + np.arange(128)[None, None, :]).reshape(-1)
    m = {"ffn_w_in_p": np.ascontiguousarray(w[..., idx]), "ffn_w_out": inp["ffn_w_out"],
         "fourier_w_out": inp["fourier_w_out"], "hgrn_w_in": inp["hgrn_w_in"], "hgrn_w_out": inp["hgrn_w_out"]}
    lb = np.stack([inp["hgrn_lb_fwd"], inp["hgrn_lb_bwd"]], 0).reshape(2, 2, 8, 128)
    m["lbcol"] = np.ascontiguousarray(lb.transpose(3, 0, 1, 2).reshape(128, 32))
    m["gncol"] = np.ascontiguousarray(inp["hgrn_norm"].T)
    m.update(host_consts())
    return m


def host_consts():
    r = np.arange(64)
    a = 2 * np.pi * np.outer(r, r) / 64
    c64, s64 = np.cos(a) / 8, np.sin(a) / 8
    z = np.zeros((64, 64))
    bd = lambda m: np.block([[m, z], [z, m]])
    k = np.arange(256)
    a = 2 * np.pi * np.outer(k, k) / 256
    c256, s256 = np.cos(a) / 16, np.sin(a) / 16
    f2 = lambda m: m.reshape(2, 128, 256).transpose(1, 0, 2).reshape(128, 512)
    fc = np.concatenate([bd(c64), bd(s64), -bd(s64), f2(c256), f2(s256), -f2(s256)], axis=1)
    s_ = np.arange(128)[:, None]
    t_ = np.arange(128)[None, :]
    same = (s_ // CH) == (t_ // CH)
    mf = (t_ // CH) * CH + MID
    mb = (t_ // CH) * CH + MID + 1
    Dmf = same * ((s_ <= t_).astype(np.float64) - (s_ <= mf))
    Dmb = same * ((s_ >= t_).astype(np.float64) - (s_ >= mb))
    Ddf = np.zeros((128, 4))
    Ddb = np.zeros((128, 4))
    sv = np.arange(128)
    for c in range(2):
        inc = (sv // CH) == c
        Ddf[:, 2 * c] = inc & (sv <= c * CH + MID)
        Ddf[:, 2 * c + 1] = inc & (sv > c * CH + MID)
        Ddb[:, 2 * c] = inc & (sv >= c * CH + MID + 1)
        Ddb[:, 2 * c + 1] = inc & (sv < c * CH + MID + 1)
    BIG = 1e30
    vf = same & (s_ <= t_)
    vb = same & (s_ >= t_)
    hcst = np.concatenate([Dmf, Ddf, Dmb, Ddb, vf * BIG, vf * -BIG, vb * BIG, vb * -BIG], axis=1)
    return {"fconst": np.ascontiguousarray(fc).astype(np.float32),
            "hconst": np.ascontiguousarray(hcst).astype(np.float32)}


def kernel(**inp):
    inp = {k: np.asarray(v) for k, v in inp.items()}
    nc = build()
    sh = host_shared(inp)
    in_maps = []
    for b in range(8):
        m = host_inputs(b, inp)
        m.update(sh)
        in_maps.append(m)
    res = run_bass_kernel_spmd(nc, in_maps, core_ids=list(range(8)))
    return np.stack([r["y"] for r in res.results], axis=0)
```
